# Optimizing a Trainium2 kernel written in Bass

```python
import math
import jax
import jax.numpy as jnp
from jax import lax
import numpy as np

D_MODEL = 1024
BATCH = 16
SEQ = 2048
DEPTH = 2

ROPE_THETA = 10000.0
NORM_EPS = 1e-6
Q_BLOCK = 128

MOBA_HEADS = 8
MOBA_HEAD_DIM = 64
MOBA_WIDTH = MOBA_HEADS * MOBA_HEAD_DIM
MOBA_BLOCK = 256
MOBA_TOPK = 3

SSD_HEADS = 8
SSD_HEAD_DIM = 64
SSD_INNER = SSD_HEADS * SSD_HEAD_DIM
SSD_GROUPS = 2
SSD_STATE = 128
SSD_CONV = 4
SSD_CHUNK = 128
SSD_XBC = SSD_INNER + 2 * SSD_GROUPS * SSD_STATE

EVEN_IN = 3 * MOBA_WIDTH + SSD_INNER + SSD_XBC + SSD_HEADS
EVEN_MIX = MOBA_WIDTH + SSD_INNER

MLA_HEADS = 16
MLA_NOPE = 64
MLA_ROPE = 32
MLA_QK = MLA_NOPE + MLA_ROPE
MLA_V = 64
MLA_Q_RANK = 512
MLA_KV_RANK = 256
MLA_IN = MLA_Q_RANK + MLA_KV_RANK + MLA_ROPE

D_FF = 2816
FFN_CONV = 3

kernel_name = "hybrid_moba_ssd_mla_convffn"


def rms_norm(x, g):
    xf = x.astype(jnp.float32)
    y = xf * lax.rsqrt(jnp.mean(xf * xf, axis=-1, keepdims=True) + NORM_EPS)
    return (y * g.astype(jnp.float32)).astype(x.dtype)


def apply_rope(x):
    d = x.shape[-1]
    inv_freq = 1.0 / (ROPE_THETA ** (jnp.arange(0, d, 2, dtype=jnp.float32) / d))
    ang = jnp.arange(x.shape[1], dtype=jnp.float32)[:, None] * inv_freq[None, :]
    cos = jnp.cos(ang)[None, :, None, :]
    sin = jnp.sin(ang)[None, :, None, :]
    xf = x.astype(jnp.float32)
    x1, x2 = xf[..., : d // 2], xf[..., d // 2:]
    return jnp.concatenate([x1 * cos - x2 * sin, x1 * sin + x2 * cos], axis=-1).astype(x.dtype)


def causal_dwconv(x, w, b):
    K = w.shape[0]
    S = x.shape[1]
    xp = jnp.pad(x, ((0, 0), (K - 1, 0), (0, 0)))
    y = b
    for k in range(K):
        y = y + xp[:, k:k + S] * w[k]
    return y


def moba_attention(q, k, v):
    bsz, S, H, dh = q.shape
    scale = dh ** -0.5
    n_blk = -(-S // MOBA_BLOCK)
    pad = n_blk * MOBA_BLOCK - S
    n_sel = min(MOBA_TOPK, n_blk)
    nqb = S // Q_BLOCK
    padw = ((0, 0), (0, pad), (0, 0), (0, 0))
    kb = jnp.pad(k, padw).reshape(bsz, n_blk, MOBA_BLOCK, H, dh).transpose(0, 3, 1, 2, 4)
    vb = jnp.pad(v, padw).reshape(bsz, n_blk, MOBA_BLOCK, H, dh).transpose(0, 3, 1, 2, 4)
    qt = q.transpose(0, 2, 1, 3)
    k_mean = jnp.mean(kb.astype(jnp.float32), axis=3)
    gate = jnp.einsum("bhsd,bhnd->bhsn", qt.astype(jnp.float32), k_mean)
    own = jnp.arange(S) // MOBA_BLOCK
    fully_past = jnp.arange(n_blk)[None, :] < own[:, None]
    gate = jnp.where(fully_past, gate, -jnp.inf)
    _, sel = lax.top_k(gate, n_sel)
    sel_ok = sel < own[None, None, :, None]

    def to_qblocks(t):
        t = t.reshape(bsz, H, nqb, Q_BLOCK, t.shape[-1]).transpose(0, 2, 1, 3, 4)
        return t.reshape(bsz * nqb, H, Q_BLOCK, t.shape[-1])

    b_ids = jnp.repeat(jnp.arange(bsz), nqb)
    j_ids = jnp.tile(jnp.arange(nqb), bsz)
    heads = jnp.arange(H)[:, None, None]
    blk_pos = jnp.arange(MOBA_BLOCK)
    q_off = jnp.arange(Q_BLOCK)

    def one_block(args):
        qi, idx, ok, b, j = args
        k_b = kb[b]
        v_b = vb[b]
        k_sel = k_b[heads, idx]
        v_sel = v_b[heads, idx]
        own_j = (j * Q_BLOCK) // MOBA_BLOCK
        k_own = lax.dynamic_index_in_dim(k_b, own_j, axis=1, keepdims=False)
        v_own = lax.dynamic_index_in_dim(v_b, own_j, axis=1, keepdims=False)
        s_sel = jnp.einsum("hqd,hqnld->hqnl", qi, k_sel, preferred_element_type=jnp.float32) * scale
        s_sel = jnp.where(ok[..., None], s_sel, -jnp.inf).reshape(H, Q_BLOCK, n_sel * MOBA_BLOCK)
        s_own = jnp.einsum("hqd,hld->hql", qi, k_own, preferred_element_type=jnp.float32) * scale
        causal = (own_j * MOBA_BLOCK + blk_pos)[None, :] <= (j * Q_BLOCK + q_off)[:, None]
        s_own = jnp.where(causal[None], s_own, -jnp.inf)
        p = jax.nn.softmax(jnp.concatenate([s_sel, s_own], axis=-1), axis=-1).astype(v.dtype)
        p_sel = p[..., : n_sel * MOBA_BLOCK].reshape(H, Q_BLOCK, n_sel, MOBA_BLOCK)
        p_own = p[..., n_sel * MOBA_BLOCK:]
        return jnp.einsum("hqnl,hqnld->hqd", p_sel, v_sel) + jnp.einsum("hql,hld->hqd", p_own, v_own)

    out = lax.map(one_block, (to_qblocks(qt), to_qblocks(sel), to_qblocks(sel_ok), b_ids, j_ids))
    out = out.reshape(bsz, nqb, H, Q_BLOCK, dh).transpose(0, 1, 3, 2, 4)
    return out.reshape(bsz, S, H, dh)


def segsum(a):
    T = a.shape[-1]
    rep = jnp.broadcast_to(a[..., :, None], a.shape + (T,))
    rep = jnp.where(jnp.tril(jnp.ones((T, T), dtype=bool), -1), rep, 0.0)
    ss = jnp.cumsum(rep, axis=-2)
    return jnp.where(jnp.tril(jnp.ones((T, T), dtype=bool)), ss, -jnp.inf)


def ssd_chunked(x, dt, A, Bm, Cm):
    bsz, S, H, P = x.shape
    G, N = Bm.shape[-2], Bm.shape[-1]
    E = H // G
    Q = SSD_CHUNK
    nc = S // Q
    xd = (x * dt[..., None]).reshape(bsz, nc, Q, G, E, P)
    a = (dt * A).reshape(bsz, nc, Q, G, E).transpose(0, 3, 4, 1, 2)
    Bc = Bm.reshape(bsz, nc, Q, G, N)
    Cc = Cm.reshape(bsz, nc, Q, G, N)
    a_cs = jnp.cumsum(a, axis=-1)
    L = jnp.exp(segsum(a))
    CB = jnp.einsum("bclgn,bcsgn->bgcls", Cc, Bc)
    y_diag = jnp.einsum("bgecls,bcsgep->bclgep", CB[:, :, None] * L, xd)
    decay_to_end = jnp.exp(a_cs[..., -1:] - a_cs).transpose(0, 3, 4, 1, 2)
    states = jnp.einsum("bclgn,bclgep->bcgepn", Bc, xd * decay_to_end[..., None])
    chunk_decay = jnp.exp(a_cs[..., -1]).transpose(3, 0, 1, 2)

    def carry_state(h, inp):
        s_c, d_c = inp
        return h * d_c[..., None, None] + s_c, h

    h0 = jnp.zeros((bsz, G, E, P, N), x.dtype)
    _, h_in = lax.scan(carry_state, h0, (states.transpose(1, 0, 2, 3, 4, 5), chunk_decay))
    decay_from_start = jnp.exp(a_cs).transpose(0, 3, 4, 1, 2)
    y_off = jnp.einsum("bclgn,cbgepn->bclgep", Cc, h_in) * decay_from_start[..., None]
    return (y_diag + y_off).reshape(bsz, S, H, P)


def moba_ssd_mixer(h, w_in, conv_w, conv_b, dt_bias, a_log, d_skip, ssd_norm, q_norm, k_norm, w_out):
    bsz, S, _ = h.shape
    proj = h @ w_in
    cuts = [MOBA_WIDTH, 2 * MOBA_WIDTH, 3 * MOBA_WIDTH, 3 * MOBA_WIDTH + SSD_INNER,
            3 * MOBA_WIDTH + SSD_INNER + SSD_XBC]
    q, k, v, z, xbc, dt = jnp.split(proj, cuts, axis=-1)
    hs = (bsz, S, MOBA_HEADS, MOBA_HEAD_DIM)
    q = apply_rope(rms_norm(q.reshape(hs), q_norm))
    k = apply_rope(rms_norm(k.reshape(hs), k_norm))
    o_attn = moba_attention(q, k, v.reshape(hs)).reshape(bsz, S, MOBA_WIDTH)
    xbc = jax.nn.silu(causal_dwconv(xbc, conv_w, conv_b))
    xs, Bm, Cm = jnp.split(xbc, [SSD_INNER, SSD_INNER + SSD_GROUPS * SSD_STATE], axis=-1)
    f32 = jnp.float32
    xs = xs.reshape(bsz, S, SSD_HEADS, SSD_HEAD_DIM).astype(f32)
    dt = jax.nn.softplus(dt.astype(f32) + dt_bias.astype(f32))
    A = -jnp.exp(a_log.astype(f32))
    y = ssd_chunked(xs, dt, A,
                    Bm.reshape(bsz, S, SSD_GROUPS, SSD_STATE).astype(f32),
                    Cm.reshape(bsz, S, SSD_GROUPS, SSD_STATE).astype(f32))
    y = y + d_skip.astype(f32)[:, None] * xs
    y = y.reshape(bsz, S, SSD_INNER).astype(h.dtype) * jax.nn.silu(z)
    gs = SSD_INNER // SSD_GROUPS
    y = rms_norm(y.reshape(bsz, S, SSD_GROUPS, gs), ssd_norm.reshape(SSD_GROUPS, gs)).reshape(bsz, S, SSD_INNER)
    return jnp.concatenate([o_attn, y], axis=-1) @ w_out


def causal_attention(q, k, v):
    bsz, S, H, dq = q.shape
    dv = v.shape[-1]
    scale = dq ** -0.5
    nqb = S // Q_BLOCK
    qb = q.reshape(bsz, nqb, Q_BLOCK, H, dq).transpose(1, 0, 2, 3, 4)
    kpos = jnp.arange(S)
    q_off = jnp.arange(Q_BLOCK)

    def one_block(args):
        qi, j = args
        s = jnp.einsum("bqhd,bkhd->bhqk", qi, k, preferred_element_type=jnp.float32) * scale
        mask = kpos[None, :] <= (j * Q_BLOCK + q_off)[:, None]
        p = jax.nn.softmax(jnp.where(mask, s, -jnp.inf), axis=-1).astype(v.dtype)
        return jnp.einsum("bhqk,bkhd->bqhd", p, v)

    o = lax.map(one_block, (qb, jnp.arange(nqb)))
    return o.transpose(1, 0, 2, 3, 4).reshape(bsz, S, H, dv)


def mla_mixer(h, w_in, q_a_norm, w_uq, kv_a_norm, w_ukv, q_norm, k_norm, w_out):
    bsz, S, _ = h.shape
    c = h @ w_in
    cq, ckv, k_pe = jnp.split(c, [MLA_Q_RANK, MLA_Q_RANK + MLA_KV_RANK], axis=-1)
    q = (rms_norm(cq, q_a_norm) @ w_uq).reshape(bsz, S, MLA_HEADS, MLA_QK)
    kv = (rms_norm(ckv, kv_a_norm) @ w_ukv).reshape(bsz, S, MLA_HEADS, MLA_NOPE + MLA_V)
    k_nope, v = kv[..., :MLA_NOPE], kv[..., MLA_NOPE:]
    k_pe = jnp.broadcast_to(k_pe[:, :, None, :], (bsz, S, MLA_HEADS, MLA_ROPE))
    k = jnp.concatenate([k_nope, k_pe], axis=-1)
    q = rms_norm(q, q_norm)
    k = rms_norm(k, k_norm)
    q = jnp.concatenate([q[..., :MLA_NOPE], apply_rope(q[..., MLA_NOPE:])], axis=-1)
    k = jnp.concatenate([k[..., :MLA_NOPE], apply_rope(k[..., MLA_NOPE:])], axis=-1)
    o = causal_attention(q, k, v)
    return o.reshape(bsz, S, MLA_HEADS * MLA_V) @ w_out


def conv_ffn(h, w_up, conv_w, conv_b, w_down):
    u = causal_dwconv(h @ w_up, conv_w, conv_b)
    g, u = jnp.split(u, 2, axis=-1)
    return (jax.nn.silu(g) * u) @ w_down


def setup_inputs(seed: int = 0) -> dict:
    key = jax.random.key(seed)
    ks = iter(jax.random.split(key, 32))
    f32 = jnp.float32
    n_even = (DEPTH + 1) // 2
    n_odd = DEPTH // 2

    def w(shape, fan_in):
        return jax.random.normal(next(ks), shape, f32) * fan_in ** -0.5

    def gain(shape):
        return 1.0 + 0.05 * jax.random.normal(next(ks), shape, f32)

    def bias(shape):
        return 0.02 * jax.random.normal(next(ks), shape, f32)

    x = jax.random.normal(next(ks), (BATCH, SEQ, D_MODEL), f32)
    mix_norm = gain((DEPTH, D_MODEL))
    ffn_norm = gain((DEPTH, D_MODEL))
    ev_w_in = w((n_even, D_MODEL, EVEN_IN), D_MODEL)
    ev_conv_w = w((n_even, SSD_CONV, SSD_XBC), SSD_CONV)
    ev_conv_b = bias((n_even, SSD_XBC))
    dt0 = jnp.exp(jax.random.uniform(next(ks), (n_even, SSD_HEADS), f32,
                                     minval=math.log(1e-3), maxval=math.log(1e-1)))
    ev_dt_bias = dt0 + jnp.log(-jnp.expm1(-dt0))
    ev_a_log = jnp.log(jax.random.uniform(next(ks), (n_even, SSD_HEADS), f32, minval=1.0, maxval=16.0))
    ev_d_skip = gain((n_even, SSD_HEADS))
    ev_ssd_norm = gain((n_even, SSD_INNER))
    ev_q_norm = gain((n_even, MOBA_HEAD_DIM))
    ev_k_norm = gain((n_even, MOBA_HEAD_DIM))
    ev_w_out = w((n_even, EVEN_MIX, D_MODEL), EVEN_MIX)
    od_w_in = w((n_odd, D_MODEL, MLA_IN), D_MODEL)
    od_q_a_norm = gain((n_odd, MLA_Q_RANK))
    od_w_uq = w((n_odd, MLA_Q_RANK, MLA_HEADS * MLA_QK), MLA_Q_RANK)
    od_kv_a_norm = gain((n_odd, MLA_KV_RANK))
    od_w_ukv = w((n_odd, MLA_KV_RANK, MLA_HEADS * (MLA_NOPE + MLA_V)), MLA_KV_RANK)
    od_q_norm = gain((n_odd, MLA_QK))
    od_k_norm = gain((n_odd, MLA_QK))
    od_w_out = w((n_odd, MLA_HEADS * MLA_V, D_MODEL), MLA_HEADS * MLA_V)
    ffn_w_up = w((DEPTH, D_MODEL, 2 * D_FF), D_MODEL)
    ffn_conv_w = w((DEPTH, FFN_CONV, 2 * D_FF), FFN_CONV)
    ffn_conv_b = bias((DEPTH, 2 * D_FF))
    ffn_w_down = w((DEPTH, D_FF, D_MODEL), D_FF)
    return {
        "x": x, "mix_norm": mix_norm, "ffn_norm": ffn_norm,
        "ev_w_in": ev_w_in, "ev_conv_w": ev_conv_w, "ev_conv_b": ev_conv_b,
        "ev_dt_bias": ev_dt_bias, "ev_a_log": ev_a_log, "ev_d_skip": ev_d_skip,
        "ev_ssd_norm": ev_ssd_norm, "ev_q_norm": ev_q_norm, "ev_k_norm": ev_k_norm,
        "ev_w_out": ev_w_out,
        "od_w_in": od_w_in, "od_q_a_norm": od_q_a_norm, "od_w_uq": od_w_uq,
        "od_kv_a_norm": od_kv_a_norm, "od_w_ukv": od_w_ukv, "od_q_norm": od_q_norm,
        "od_k_norm": od_k_norm, "od_w_out": od_w_out,
        "ffn_w_up": ffn_w_up, "ffn_conv_w": ffn_conv_w, "ffn_conv_b": ffn_conv_b,
        "ffn_w_down": ffn_w_down,
    }


def reference(x, mix_norm, ffn_norm,
              ev_w_in, ev_conv_w, ev_conv_b, ev_dt_bias, ev_a_log, ev_d_skip,
              ev_ssd_norm, ev_q_norm, ev_k_norm, ev_w_out,
              od_w_in, od_q_a_norm, od_w_uq, od_kv_a_norm, od_w_ukv, od_q_norm,
              od_k_norm, od_w_out,
              ffn_w_up, ffn_conv_w, ffn_conv_b, ffn_w_down):
    for layer in range(DEPTH):
        i = layer // 2
        hn = rms_norm(x, mix_norm[layer])
        if layer % 2 == 0:
            x = x + moba_ssd_mixer(hn, ev_w_in[i], ev_conv_w[i], ev_conv_b[i], ev_dt_bias[i],
                                   ev_a_log[i], ev_d_skip[i], ev_ssd_norm[i], ev_q_norm[i],
                                   ev_k_norm[i], ev_w_out[i])
        else:
            x = x + mla_mixer(hn, od_w_in[i], od_q_a_norm[i], od_w_uq[i], od_kv_a_norm[i],
                              od_w_ukv[i], od_q_norm[i], od_k_norm[i], od_w_out[i])
        x = x + conv_ffn(rms_norm(x, ffn_norm[layer]), ffn_w_up[layer], ffn_conv_w[layer],
                         ffn_conv_b[layer], ffn_w_down[layer])
    return x
```

```python
import math
import os
import numpy as np
import ml_dtypes
import concourse.bass as bass
import concourse.mybir as mybir
from concourse.bass_utils import run_bass_kernel_spmd

F32 = mybir.dt.float32
BF16 = mybir.dt.bfloat16
ALU = mybir.AluOpType
AF = mybir.ActivationFunctionType
AX = mybir.AxisListType

S = 2048
D = 1024
NSEQ = 2
NEG = -30000.0
EPS = 1e-6
DFF = 2816


class T:
    def __init__(self, name, ap):
        self.name = name
        self.ap = ap

    def __getitem__(self, idx):
        return self.ap[idx]


class _Op:
    __slots__ = ("eng", "fn", "deps", "is_dma", "lane", "target", "sig", "sigidx", "idx")


class _Ent:
    __slots__ = ("writer", "readers")

    def __init__(self, writer=None, readers=None):
        self.writer = writer
        self.readers = readers if readers is not None else []


class KB:
    ENGS = ("pe", "act", "dve", "pool", "sp")
    ARENA = 53200

    def __init__(self, nc):
        self.nc = nc
        self.ops = []
        self.eng_ops = {e: [] for e in self.ENGS}
        self.track = {}
        self.lanes = {}
        self.arena = nc.alloc_sbuf_tensor("arena", [128, self.ARENA], F32)
        self.off = 0
        self.base = 0
        self.pname = ""
        self.nsb = 0
        self.banks = [T(f"pb{i}", nc.alloc_psum_tensor(f"pb{i}", [128, 512], F32)[:, :]) for i in range(8)]

    def sb(self, name, shape, dt=F32, persist=False):
        n = 1
        for d in shape[1:]:
            n *= d
        words = n if dt == F32 else (n + 1) // 2
        words = (words + 7) // 8 * 8
        assert self.off + words <= self.ARENA, f"SBUF arena overflow at {name}: {self.off}+{words}"
        ap = self.arena[0:shape[0], self.off:self.off + words]
        if dt != F32:
            ap = ap.bitcast(dt)
        ap = ap[:, 0:n]
        if len(shape) == 3:
            ap = ap.rearrange("p (a b) -> p a b", a=shape[1])
        elif len(shape) == 4:
            ap = ap.rearrange("p (a b c) -> p a b c", a=shape[1], b=shape[2])
        self.off += words
        self.nsb += 1
        return T(f"{self.pname}.{name}.{self.nsb}", ap)

    def phase(self, name):
        self.barrier()
        self.pname = name
        self.off = self.base

    def persist_mark(self):
        self.base = self.off

    @staticmethod
    def _norm(acc):
        out = []
        for a in acc:
            if isinstance(a, tuple):
                out.append((a[0].name, a[1]))
            else:
                out.append((a.name, None))
        return out

    def _ents(self, buf, key):
        d = self.track.setdefault(buf, {None: _Ent()})
        if key is None:
            return list(d.values())
        if key not in d:
            base = d[None]
            d[key] = _Ent(base.writer, list(base.readers))
        return [d[key]]

    def _record(self, op, reads, writes):
        deps = set()
        reads = self._norm(reads)
        writes = self._norm(writes)
        wn = {w[0] for w in writes}
        writes = writes + [r for r in reads if r[0].startswith("pb") and r[0] not in wn]
        for buf, key in reads:
            for e in self._ents(buf, key):
                if e.writer is not None:
                    deps.add(e.writer)
        for buf, key in writes:
            for e in self._ents(buf, key):
                if e.writer is not None:
                    deps.add(e.writer)
                deps.update(e.readers)
        for buf, key in reads:
            for e in self._ents(buf, key):
                e.readers.append(op)
        for buf, key in writes:
            if key is None:
                self.track[buf] = {None: _Ent(op, [])}
            else:
                e = self._ents(buf, key)[0]
                e.writer = op
                e.readers = []
        deps.discard(op)
        return deps

    def _new(self, eng, fn):
        o = _Op()
        o.eng = eng
        o.fn = fn
        o.is_dma = False
        o.lane = None
        o.target = 0
        o.sig = False
        o.sigidx = 0
        o.idx = len(self.ops)
        o.deps = set()
        return o

    def op(self, eng, fn, reads=(), writes=()):
        o = self._new(eng, fn)
        o.deps = self._record(o, reads, writes)
        self.ops.append(o)
        self.eng_ops[eng].append(o)
        return o

    def dma(self, eng, out, in_, reads=(), writes=(), lane=None, group=False, **kw):
        o = self._new(eng, lambda e, out=out, in_=in_, kw=kw: e.dma_start(out=out, in_=in_, **kw))
        o.is_dma = True
        lane = f"{eng}_{lane}"
        ln = self.lanes.setdefault(lane, {"sem": None, "count": 0, "last": None, "gprev": None})
        o.lane = lane
        o.deps = self._record(o, reads, writes)
        if not group:
            ln["gprev"] = ln["last"]
        if ln["gprev"] is not None:
            o.deps.add(ln["gprev"])
        ln["count"] += 1
        o.target = 16 * ln["count"]
        ln["last"] = o
        self.ops.append(o)
        self.eng_ops[eng].append(o)
        return o

    def barrier(self):
        lasts = set()
        for e in self.ENGS:
            for o in reversed(self.eng_ops[e]):
                if o.fn is not None and not o.is_dma:
                    lasts.add(o)
                    break
        for ln in self.lanes.values():
            if ln["last"] is not None:
                lasts.add(ln["last"])
        if not lasts:
            return
        for e in self.ENGS:
            o = self._new(e, None)
            o.deps = set(lasts)
            self.ops.append(o)
            self.eng_ops[e].append(o)

    def act(self, out, in_, func, reads, writes, **kw):
        return self.op("act", lambda e: e.activation(out=out, in_=in_, func=func, **kw), reads, writes)

    def tt(self, eng, out, in0, in1, op, reads, writes):
        return self.op(eng, lambda e: e.tensor_tensor(out=out, in0=in0, in1=in1, op=op), reads, writes)

    def ts(self, eng, out, in0, s1, s2, op0, op1, reads, writes):
        if op1 is None:
            return self.op(eng, lambda e: e.tensor_scalar(out=out, in0=in0, scalar1=s1, scalar2=None, op0=op0), reads, writes)
        return self.op(eng, lambda e: e.tensor_scalar(out=out, in0=in0, scalar1=s1, scalar2=s2, op0=op0, op1=op1), reads, writes)

    def stt(self, out, in0, scalar, in1, op0, op1, reads, writes):
        return self.op("dve", lambda e: e.scalar_tensor_tensor(out=out, in0=in0, scalar=scalar, in1=in1, op0=op0, op1=op1), reads, writes)

    def copy(self, eng, out, in_, reads, writes):
        if eng == "act":
            return self.op("act", lambda e: e.copy(out=out, in_=in_), reads, writes)
        return self.op(eng, lambda e: e.tensor_copy(out=out, in_=in_), reads, writes)

    def memset(self, eng, ap, val, writes):
        return self.op(eng, lambda e: e.memset(ap, val), (), writes)

    def mm(self, out, lhsT, rhs, start, stop, reads, writes, sgc=False):
        return self.op("pe", lambda e: e.matmul(out, lhsT=lhsT, rhs=rhs, start=start, stop=stop, skip_group_check=sgc), reads, writes)

    def tr(self, out, in_, ident, reads, writes):
        return self.op("pe", lambda e: e.transpose(out=out, in_=in_, identity=ident), reads, writes)

    def emit(self):
        nc = self.nc
        for o in self.ops:
            best = {}
            nd = set()
            for d in o.deps:
                if d.is_dma:
                    nd.add(d)
                else:
                    if d.eng == "pe" and o.eng == "pe" and not o.is_dma and o.fn is not None:
                        continue
                    b = best.get(d.eng)
                    if b is None or d.idx > b.idx:
                        best[d.eng] = d
            nd.update(best.values())
            o.deps = nd
            for d in nd:
                if not d.is_dma:
                    assert d.fn is not None
                    d.sig = True
        for e in self.ENGS:
            c = 0
            for o in self.eng_ops[e]:
                if o.sig and not o.is_dma:
                    c += 1
                    o.sigidx = c
        sems = {e: nc.alloc_semaphore(name=f"sem_{e}") for e in self.ENGS}
        for name, ln in self.lanes.items():
            ln["sem"] = nc.alloc_semaphore(name=f"lane_{name}")
        with nc.Block() as block:
            for e in self.ENGS:
                ops = self.eng_ops[e]
                final = (e == "sp")

                def body(h, e=e, ops=ops, final=final):
                    waited = {}
                    for o in ops:
                        for d in sorted(o.deps, key=lambda d: d.idx):
                            if d.is_dma:
                                s, v = self.lanes[d.lane]["sem"], d.target
                            else:
                                s, v = sems[d.eng], d.sigidx
                            kk = s.num
                            if waited.get(kk, 0) >= v:
                                continue
                            waited[kk] = v
                            h.wait_ge(s, v)
                        if o.fn is None:
                            continue
                        inst = o.fn(h)
                        if o.is_dma:
                            inst.then_inc(self.lanes[o.lane]["sem"], 16)
                        elif o.sig:
                            inst.then_inc(sems[e], 1)
                    if final:
                        for name, ln in self.lanes.items():
                            if ln["count"]:
                                h.wait_ge(ln["sem"], 16 * ln["count"])

                dec = {"pe": block.tensor, "act": block.scalar, "dve": block.vector,
                       "pool": block.gpsimd, "sp": block.sync}[e]
                dec(body)


def pipeline(n, stages):
    maxoff = max(o for o, _ in stages)
    for t in range(n + maxoff):
        for off, fn in stages:
            i = t - off
            if 0 <= i < n:
                fn(i)


def bview(ap, pattern):
    return bass.AP(tensor=ap.tensor, offset=ap.offset, ap=[list(ap.ap[0])] + [list(p) for p in pattern])


def host_consts():
    c = {}
    c["ident"] = np.eye(128, dtype=np.float32)
    pos = np.arange(S, dtype=np.float32)
    inv0 = (1.0 / (10000.0 ** (np.arange(0, 64, 2, dtype=np.float32) / 64))).astype(np.float32)
    ang0 = pos[None, :] * inv0[:, None]
    idx = (np.arange(128) % 64) % 32
    c["cos0"] = np.cos(ang0)[idx].astype(np.float32)
    c["sin0"] = np.sin(ang0)[idx].astype(np.float32)
    R0 = np.zeros((128, 128), np.float32)
    for m in range(128):
        if (m % 64) < 32:
            R0[m + 32, m] = -1.0
        else:
            R0[m - 32, m] = 1.0
    c["R0"] = R0
    bo = np.zeros((128, 128), np.float32)
    bo[:64, :64] = 1.0
    bo[64:, 64:] = 1.0
    c["bones"] = bo
    inv1 = (1.0 / (10000.0 ** (np.arange(0, 32, 2, dtype=np.float32) / 32))).astype(np.float32)
    ang1 = pos[None, :] * inv1[:, None]
    cos1 = np.zeros((128, S), np.float32)
    cos1[:64] = 1.0
    sin1 = np.zeros((128, S), np.float32)
    cos1[64:96] = np.cos(ang1)[np.arange(32) % 16]
    sin1[64:96] = np.sin(ang1)[np.arange(32) % 16]
    c["cos1"] = cos1
    c["sin1"] = sin1
    R1 = np.zeros((96, 96), np.float32)
    for m in range(64, 96):
        if m - 64 < 16:
            R1[m + 16, m] = -1.0
        else:
            R1[m - 16, m] = 1.0
    R1p = np.zeros((128, 128), np.float32)
    R1p[:96, :96] = R1
    c["R1"] = R1p
    o96 = np.zeros((128, 128), np.float32)
    o96[:96, :96] = 1.0
    c["ones96"] = o96
    sh = np.zeros((128, 128), np.float32)
    for i in range(32):
        sh[i, 64 + i] = 1.0
    c["shpe"] = sh
    rr_h = np.arange(128) // 32
    nn_h = np.arange(128) // 64
    for m in range(2):
        ind = (rr_h[:, None] == (2 * m + nn_h)[None, :]).astype(np.float32)
        c[f"indRN{m}"] = ind
        c[f"indNR{m}"] = np.ascontiguousarray(ind.T)
    c["bones32"] = (rr_h[:, None] == rr_h[None, :]).astype(np.float32)
    R32 = np.zeros((128, 128), np.float32)
    for m in range(128):
        if (m % 32) < 16:
            R32[m + 16, m] = -1.0
        else:
            R32[m - 16, m] = 1.0
    c["R32"] = R32
    rep32 = np.zeros((128, 128), np.float32)
    for m in range(128):
        rep32[m % 32, m] = 1.0
    c["rep32"] = rep32
    c["cosR"] = np.cos(ang1)[(np.arange(128) % 32) % 16].astype(np.float32)
    c["sinR"] = np.sin(ang1)[(np.arange(128) % 32) % 16].astype(np.float32)
    kk = np.arange(128)[:, None, None]
    rr = np.arange(4)[None, :, None]
    jj = np.arange(512)[None, None, :]
    c["cmask"] = np.where(rr * 128 + kk <= jj, 0.0, NEG).astype(np.float32)
    Z = np.zeros((64, 64 * 128), np.float32)
    for p in range(64):
        Z[p, p * 128:(p + 1) * 128] = 1.0
    c["Zsel"] = Z
    tt = np.arange(16)[:, None]
    nn = (np.arange(64) % 8)[None, :]
    own = tt // 2
    c["negm"] = np.where(nn < own, 0.0, -1e30).astype(np.float32).reshape(1, 16 * 64)
    c["elig"] = (nn < own).astype(np.float32).reshape(1, 16 * 64)
    c["ownm"] = (nn == own).astype(np.float32).reshape(1, 16 * 64)
    d_ = np.arange(128)[:, None]
    l_ = np.arange(128)[None, :]
    c["triLE"] = (d_ <= l_).astype(np.float32)
    c["strict"] = (d_ > l_).astype(np.float32)
    c["tribias"] = np.tile(np.where(d_ > l_, NEG, 0.0).astype(np.float32), (1, 4))
    c["onesF"] = np.ones((128, 128), np.float32)
    return c


def host_params(inp):
    p = {}
    f = lambda a: np.ascontiguousarray(np.asarray(a, dtype=np.float32))
    for l in range(2):
        p[f"mixg{l}"] = f(inp["mix_norm"][l].reshape(1, D))
        p[f"ffng{l}"] = f(inp["ffn_norm"][l].reshape(1, D))
        p[f"fcw{l}"] = f(inp["ffn_conv_w"][l].T.reshape(44, 128, 3).transpose(1, 0, 2))
        p[f"fcb{l}"] = f(inp["ffn_conv_b"][l].reshape(44, 128).T)
        p[f"wup{l}"] = f(inp["ffn_w_up"][l])
        p[f"wdn{l}"] = f(inp["ffn_w_down"][l])
    p["w_in0"] = f(inp["ev_w_in"][0])
    p["w_out0"] = f(inp["ev_w_out"][0])
    p["gq0"] = f(np.tile(inp["ev_q_norm"][0], 2).reshape(128, 1))
    p["gk0"] = f(np.tile(inp["ev_k_norm"][0], 2).reshape(128, 1))
    p["cw0"] = f(inp["ev_conv_w"][0].T.reshape(8, 128, 4).transpose(1, 0, 2))
    p["cb0"] = f(inp["ev_conv_b"][0].reshape(8, 128).T)
    p["dtb"] = f(inp["ev_dt_bias"][0].reshape(1, 8))
    p["alog"] = f(inp["ev_a_log"][0].reshape(1, 8))
    p["dskip"] = f(inp["ev_d_skip"][0].reshape(1, 8))
    p["ssdg"] = f(inp["ev_ssd_norm"][0].reshape(1, 512))
    p["w_in1"] = f(inp["od_w_in"][0])
    p["w_uq"] = f(inp["od_w_uq"][0])
    p["w_ukv"] = f(inp["od_w_ukv"][0])
    p["w_out1"] = f(inp["od_w_out"][0])
    p["gqa"] = f(inp["od_q_a_norm"][0].reshape(4, 128).T)
    p["gkva"] = f(inp["od_kv_a_norm"][0].reshape(2, 128).T)
    for nm, src in (("q", inp["od_q_norm"][0]), ("k", inp["od_k_norm"][0])):
        p[f"g{nm}N"] = f(np.tile(src[0:64], 2).reshape(128, 1))
        p[f"g{nm}R"] = f(np.tile(src[64:96], 4).reshape(128, 1))
    return p


def norm_stats(k, C, xs, t, gb, hn):
    k.act(hn[:, :], xs[:, t, :], AF.Square, [(xs, t)], [hn, (C["ss"], t)], accum_out=C["ss"][:, t:t + 1])
    k.act(C["rs"][:, t:t + 1], C["ss"][:, t:t + 1], AF.Ln, [(C["ss"], t), C["eps"]], [(C["rs"], t)], scale=1.0 / D, bias=C["eps"][:, :])
    k.act(C["rs"][:, t:t + 1], C["rs"][:, t:t + 1], AF.Exp, [(C["rs"], t)], [(C["rs"], t)], scale=-0.5)
    k.stt(hn[:, :], xs[:, t, :], C["rs"][:, t:t + 1], gb[:, :], ALU.mult, ALU.mult, [(xs, t), (C["rs"], t), gb], [hn])


def norm_tr(k, C, hn, hnT, t):
    ptr = k.banks[7]
    pv = ptr.ap.bitcast(BF16)
    for c in range(8):
        k.tr(pv[:, c * 128:(c + 1) * 128], hn[:, c * 128:(c + 1) * 128], C["idb"][:, :], [hn, C["idb"]], [ptr])
    k.copy("act", hnT[:, :, t * 128:(t + 1) * 128], pv.rearrange("p (c n) -> p c n", c=8), [ptr], [(hnT, t)])


def norm_T(k, C, xs, ntile, gb, hnT):
    ptr = k.banks[7]
    pv = ptr.ap.bitcast(BF16)
    for t in range(ntile):
        hn = C["hn"][t % 2]
        k.act(C["junk"][:, :], xs[:, t, :], AF.Square, [(xs, t)], [C["junk"], (C["ss"], t)], accum_out=C["ss"][:, t:t + 1])
        k.act(C["rs"][:, t:t + 1], C["ss"][:, t:t + 1], AF.Ln, [(C["ss"], t), C["eps"]], [(C["rs"], t)], scale=1.0 / D, bias=C["eps"][:, :])
        k.act(C["rs"][:, t:t + 1], C["rs"][:, t:t + 1], AF.Exp, [(C["rs"], t)], [(C["rs"], t)], scale=-0.5)
        k.stt(hn[:, :], xs[:, t, :], C["rs"][:, t:t + 1], gb[:, :], ALU.mult, ALU.mult, [(xs, t), (C["rs"], t), gb], [hn])
        for c in range(8):
            k.tr(pv[:, c * 128:(c + 1) * 128], hn[:, c * 128:(c + 1) * 128], C["idb"][:, :], [hn, C["idb"]], [ptr])
        k.copy("act", hnT[:, :, t * 128:(t + 1) * 128], pv.rearrange("p (c n) -> p c n", c=8), [ptr], [(hnT, t)])


def common_consts(k, G):
    C = {}
    C["idb"] = k.sb("idb", [128, 128], BF16)
    k.dma("pool", C["idb"][:, :], G["ident"][:, :], writes=[C["idb"]], lane="c0")
    C["eps"] = k.sb("eps", [128, 1])
    k.memset("dve", C["eps"][:, :], EPS, [C["eps"]])
    C["one"] = k.sb("one", [128, 1])
    k.memset("dve", C["one"][:, :], 1.0, [C["one"]])
    C["ss"] = k.sb("ss", [128, 4])
    C["rs"] = k.sb("rs", [128, 4])
    C["hn"] = [k.sb(f"hn{i}", [128, 1024], BF16) for i in range(2)]
    k.persist_mark()
    return C


def load_w(k, dst, src, nchunk, lane, col0=None, col1=None):
    for c in range(nchunk):
        s = src[c * 128:(c + 1) * 128, :] if col0 is None else src[c * 128:(c + 1) * 128, col0:col1]
        k.dma("pool", dst[:, c, :], s, writes=[dst], lane=lane, group=(c > 0))


def load_w_blocks(k, dst, src, nchunk, blocks, lane):
    for bi, (c0, c1) in enumerate(blocks):
        k.dma("pool", dst[:, :, c0:c1], src[:, c0:c1].rearrange("(c p) n -> p c n", p=128), writes=[(dst, bi)], lane=f"{lane}_{bi}")


def phase_A0(k, C, G):
    k.phase("A0")
    Win = k.sb("Win", [128, 8, 3080], BF16)
    wblocks = [(0, 512), (512, 1024), (2048, 2560), (2560, 3072), (1024, 1536), (1536, 2048), (3072, 3080)]
    load_w_blocks(k, Win, G["w_in0"], 8, wblocks, "wi")
    wblk = lambda col: [i for i, (a, b_) in enumerate(wblocks) if a <= col < b_][0]
    gb = k.sb("gb", [128, D])
    k.dma("sp", gb[:, :], G["mixg0"][:, :].broadcast_to([128, D]), writes=[gb], lane="c1")
    cosT = k.sb("cosT", [128, S]); sinT = k.sb("sinT", [128, S])
    k.dma("sp", cosT[:, :], G["cos0"][:, :], writes=[cosT], lane="c2")
    k.dma("sp", sinT[:, :], G["sin0"][:, :], writes=[sinT], lane="c3")
    R0 = k.sb("R0", [128, 128], BF16); bones = k.sb("bones", [128, 128], BF16)
    k.dma("pool", R0[:, :], G["R0"][:, :], writes=[R0], lane="c4")
    k.dma("pool", bones[:, :], G["bones"][:, :], writes=[bones], lane="c5")
    gq = k.sb("gq", [128, 1]); gk = k.sb("gk", [128, 1]); cw = k.sb("cw", [128, 8, 4]); cb = k.sb("cb", [128, 8])
    k.dma("sp", gq[:, :], G["gq0"][:, :], writes=[gq], lane="c1")
    k.dma("sp", gk[:, :], G["gk0"][:, :], writes=[gk], lane="c2")
    k.dma("sp", cw[:, :, :], G["cw0"][:, :, :], writes=[cw], lane="c3")
    k.dma("sp", cb[:, :], G["cb0"][:, :], writes=[cb], lane="c1")
    rawp = k.sb("rawp", [128, 8, 515])
    xs2 = [k.sb(f"xs{i}", [128, 4, D]) for i in range(2)]
    hnT2 = [k.sb(f"hnT{i}", [128, 8, 512], BF16) for i in range(2)]
    hn4 = C["hn"] + [k.sb(f"hnx{i}", [128, 1024], BF16) for i in range(2)]
    NW = 4
    W2 = lambda n, dt=F32, nb=4: [k.sb(f"{n}{i}", [128, 512], dt) for i in range(nb)]
    sq = W2("sq", BF16); rstd = W2("rstd"); qn = W2("qn", BF16); t1 = W2("t1", BF16); t2 = W2("t2", BF16); qo = W2("qo", BF16)
    acc = W2("acc"); xo = W2("xo", BF16); zo = W2("zo", BF16, 2)
    vo = [k.sb(f"vo{i}", [128, 8, 65], BF16) for i in range(2)]
    for b_ in range(2):
        k.memset("pool", vo[b_][:, :, 64:65], 1.0, [vo[b_]])
    dto = [k.sb(f"dto{i}", [128, 8]) for i in range(2)]
    pb = k.banks
    it = 0
    NCH = 2 * NSEQ * 2

    def xload(i):
        s, pos = i // 4, (i % 4) * 512
        k.dma("sp", xs2[i % 2][:, :, :], G["x"][s, pos:pos + 512, :].rearrange("(t p) d -> p t d", p=128), writes=[xs2[i % 2]], lane=f"xa{i % 2}")

    xload(0)
    for t in range(4):
        norm_stats(k, C, xs2[0], t, gb, hn4[t])
        norm_tr(k, C, hn4[t], hnT2[0], t)
    for i in range(NCH):
        s, pos = i // 4, (i % 4) * 512
        hnT = hnT2[i % 2]
        base = it
        it += 16

        def st0(cc, s=s, pos=pos, base=base):
            col0 = cc * 128 if cc < 8 else 2048 + (cc - 8) * 128
            acc_b = pb[(base + cc) % 3]
            b = (base + cc) % NW
            for c in range(8):
                k.mm(acc_b[:, :], Win[:, c, col0:col0 + 128], hnT[:, c, :], c == 0, c == 7, [(Win, wblk(col0)), hnT], [acc_b])
            if cc < 8:
                k.act(sq[b][:, :], acc_b[:, :], AF.Square, [acc_b], [sq[b]])
            else:
                j = cc - 8
                rp = rawp
                if pos == 0:
                    k.memset("pool", rp[:, j, 0:3], 0.0, [(rp, j)])
                else:
                    k.copy("pool", rp[:, j, 0:3], rp[:, j, 512:515], [(rp, j)], [(rp, j)])
                k.copy("act", rp[:, j, 3:515], acc_b[:, :], [acc_b], [(rp, j)])
                k.act(acc[b][:, :], acc_b[:, :], AF.Identity, [acc_b, cw, cb], [acc[b]], scale=cw[:, j, 3:4], bias=cb[:, j:j + 1])

        def st1(cc, s=s, pos=pos, base=base):
            acc_b = pb[(base + cc) % 3]
            b = (base + cc) % NW
            if cc < 8:
                g = gq if cc < 4 else gk
                pss = pb[3 + (base + cc) % 2]
                k.mm(pss[:, :], bones[:, :], sq[b][:, :], True, True, [bones, sq[b]], [pss])
                k.act(rstd[b][:, :], pss[:, :], AF.Ln, [pss, C["eps"]], [rstd[b]], scale=1.0 / 64, bias=C["eps"][:, :])
                k.act(rstd[b][:, :], rstd[b][:, :], AF.Exp, [rstd[b]], [rstd[b]], scale=-0.5)
                k.stt(qn[b][:, :], acc_b[:, :], g[:, 0:1], rstd[b][:, :], ALU.mult, ALU.mult, [acc_b, g, rstd[b]], [qn[b]])
            else:
                j = cc - 8
                rp = rawp
                for q_ in range(0, 3):
                    k.stt(acc[b][:, :], rp[:, j, q_:q_ + 512], cw[:, j, q_:q_ + 1], acc[b][:, :], ALU.mult, ALU.add, [(rp, j), cw, acc[b]], [acc[b]])
                k.act(xo[b][:, :], acc[b][:, :], AF.Silu, [acc[b]], [xo[b]])
                k.dma("sp", G["xbcT"][s, j * 128:(j + 1) * 128, pos:pos + 512], xo[b][:, :], reads=[xo[b]], writes=[(G["xbcT"], (s, j, pos))], lane=f"xo{b}")

        def st2(cc, s=s, pos=pos, base=base):
            if cc >= 8:
                return
            b = (base + cc) % NW
            dst = (G["qT0"] if cc < 4 else G["kT0"])
            j = cc % 4
            prx = pb[5 + (base + cc) % 2]
            k.mm(prx[:, :], R0[:, :], qn[b][:, :], True, True, [R0, qn[b]], [prx])
            k.tt("pool", t1[b][:, :], qn[b][:, :], cosT[:, pos:pos + 512], ALU.mult, [qn[b], cosT], [t1[b]])
            k.tt("dve", t2[b][:, :], prx[:, :], sinT[:, pos:pos + 512], ALU.mult, [prx, sinT], [t2[b]])
            k.tt("dve", qo[b][:, :], t1[b][:, :], t2[b][:, :], ALU.add, [t1[b], t2[b]], [qo[b]])
            k.dma("sp", dst[s, j * 128:(j + 1) * 128, pos:pos + 512], qo[b][:, :], reads=[qo[b]], writes=[(dst, (s, j, pos))], lane=f"qo{b}")

        pipeline(16, [(0, st0), (1, st1), (2, st2)])
        if i + 1 < NCH:
            xload(i + 1)
            for t in range(4):
                norm_stats(k, C, xs2[(i + 1) % 2], t, gb, hn4[t])
        for t in range(4):
            b = t % 2
            r0 = pos + t * 128
            pv_, pz_, pd_ = pb[3 + t % 2], pb[5 + t % 2], pb[7]
            for c in range(8):
                k.mm(pv_[:, :], hnT[:, c, t * 128:(t + 1) * 128], Win[:, c, 1024:1536], c == 0, c == 7, [hnT, (Win, 4)], [pv_])
            for c in range(8):
                k.mm(pz_[:, :], hnT[:, c, t * 128:(t + 1) * 128], Win[:, c, 1536:2048], c == 0, c == 7, [hnT, (Win, 5)], [pz_])
            for c in range(8):
                k.mm(pd_[:, 0:8], hnT[:, c, t * 128:(t + 1) * 128], Win[:, c, 3072:3080], c == 0, c == 7, [hnT, (Win, 6)], [pd_])
            k.copy("dve", vo[b][:, :, 0:64], pv_[:, :].rearrange("p (h d) -> p h d", h=8), [pv_], [vo[b]])
            k.act(zo[b][:, :], pz_[:, :], AF.Silu, [pz_], [zo[b]])
            k.copy("dve", dto[b][:, :], pd_[:, 0:8], [pd_], [dto[b]])
            k.dma("sp", G["v0"][s, r0:r0 + 128, :], vo[b][:, :, :].rearrange("p h d -> p (h d)"), reads=[vo[b]], writes=[(G["v0"], (s, r0))], lane=f"vo{b}")
            k.dma("sp", G["z0"][s, r0:r0 + 128, :], zo[b][:, :], reads=[zo[b]], writes=[(G["z0"], (s, r0))], lane=f"zo{b}")
            k.dma("sp", G["dt0"][s, :, r0 // 128, :], dto[b][:, :], reads=[dto[b]], writes=[(G["dt0"], (s, r0))], lane=f"do{b}")
        if i + 1 < NCH:
            for t in range(4):
                norm_tr(k, C, hn4[t], hnT2[(i + 1) % 2], t)


def attention_run(k, C, heads, scale, cmask, Pt):
    pb = k.banks
    steps = [(hd, c, kt) for hd in heads for c in range(4) for kt in range(4 * c + 4)]
    st = {"si": 0, "oi": 0}

    def emit_S(i):
        hd, c, kt = steps[i]
        r = kt - 4 * c
        j0 = max(r, 0) * 128
        Sb = pb[i % 3]
        bias = hd.get("bias")
        k.mm(Sb[:, j0:512], hd["k"](kt), hd["q"](c, j0), True, bias is None and r < 0, [hd["kT"], hd["qT"]], [Sb])
        if bias is not None:
            Z, biasT, hsel = bias
            zi = (hsel * 8 + kt // 2) * 128
            k.mm(Sb[:, j0:512], Z[:, zi:zi + 128], biasT[:, c * 512 + j0:(c + 1) * 512], False, r < 0, [Z, biasT], [Sb])
        if r >= 0:
            k.mm(Sb[:, j0:512], C["idb"][:, :], cmask[:, r, j0:512], False, True, [C["idb"], cmask], [Sb])

    emit_S(0)
    for i, (hd, c, kt) in enumerate(steps):
        if i + 1 < len(steps):
            emit_S(i + 1)
        r = kt - 4 * c
        j0 = max(r, 0) * 128
        Sb = pb[i % 3]
        P = Pt[i % 3]
        if kt == 0:
            st["oi"] += 1
        O = pb[3 + st["oi"] % 2]
        Ov = O.ap[:, 0:260].rearrange("p (a b) -> p a b", a=4)
        k.act(P[:, j0:512], Sb[:, j0:512], AF.Exp, [Sb], [P], scale=scale)
        for qi in range(max(r, 0), 4):
            k.mm(Ov[:, qi, :], P[:, qi * 128:(qi + 1) * 128], hd["v"](kt), kt == 0 and qi == 0, kt == 4 * c + qi, [P, hd["Va"]], [O], sgc=True)
        if kt == 4 * c + 3:
            rec = C["rec"][st["oi"] % 2]
            omix, ocol = hd["omix"], hd["ocol"]
            k.op("dve", lambda e, rec=rec, Ov=Ov: e.reciprocal(out=rec[:, :], in_=Ov[:, :, 64]), [O], [rec])
            k.tt("dve", omix[:, 4 * c:4 * c + 4, ocol:ocol + 64], Ov[:, :, 0:64], bview(rec.ap, [[1, 4], [0, 64]]), ALU.mult,
                 [O, rec], [(omix, (c, ocol))])
            if hd.get("after") is not None and c == 3:
                hd["after"]()


def phase_B0(k, C, G):
    k.phase("B0")
    pb = k.banks
    kTs = [k.sb(f"kTz{i}", [128, 8, S], BF16) for i in range(2)]
    qTs = [k.sb(f"qT{i}", [128, 4, S], BF16) for i in range(2)]
    biasTs = [k.sb(f"biasT{i}", [64, S], BF16) for i in range(2)]
    Vaug = k.sb("Vaug", [128, 16, 8, 65], BF16)
    Z = k.sb("Z", [64, 64 * 128], BF16)
    k.dma("pool", Z[:, :], G["Zsel"][:, :], writes=[Z], lane="c1")
    cmask = k.sb("cmask", [128, 4, 512], BF16)
    k.dma("pool", cmask[:, :, :], G["cmask"][:, :, :], writes=[cmask], lane="c2")
    negm = k.sb("negm", [128, 16, 64]); elig = k.sb("elig", [128, 16, 64]); ownm = k.sb("ownm", [128, 16, 64])
    for t_, nm, ln in ((negm, "negm", "c3"), (elig, "elig", "c4"), (ownm, "ownm", "c5")):
        k.dma("sp", t_[:, :, :].rearrange("p a b -> p (a b)"), G[nm][:, :].broadcast_to([128, 1024]), writes=[t_], lane=ln)
    km = k.sb("km", [128, 8, 8]); kmb = k.sb("kmb", [128, 8, 8], BF16)
    Gs = [k.sb(f"G{i}", [128, 128, 8]) for i in range(3)]
    m_ = k.sb("m", [128, 128]); tmask = k.sb("tmask", [128, 128, 8]); bq = k.sb("bq", [128, 16, 128], BF16)
    Pt = [k.sb(f"Pt{i}", [128, 512], BF16) for i in range(3)]
    omix = k.sb("omix", [128, 16, 512], BF16)
    C["rec"] = [k.sb(f"rec{i}", [128, 4]) for i in range(2)]
    k.memset("pool", bq[:, :, :], 0.0, [bq])
    for b_ in range(2):
        k.memset("pool", kTs[b_][:, :, :], 0.0, [kTs[b_]])

    def load_kq(s):
        kT, qT = kTs[s % 2], qTs[s % 2]
        for cc in range(4):
            k.dma("sp", kT[0:64, 2 * cc, :], G["kT0"][s, cc * 128:cc * 128 + 64, :], reads=[G["kT0"]], writes=[kT], lane=f"kT{s % 2}", group=(cc > 0))
            k.dma("sp", kT[64:128, 2 * cc + 1, :], G["kT0"][s, cc * 128 + 64:(cc + 1) * 128, :], reads=[G["kT0"]], writes=[kT], lane=f"kT{s % 2}", group=True)
        k.dma("sp", qT[:, :, :], G["qT0"][s].rearrange("(c p) t -> p c t", p=128), reads=[G["qT0"]], writes=[qT], lane=f"qT{s % 2}")

    def load_v(s):
        k.dma("sp", Vaug[:, :, :, :].rearrange("p t h d -> p t (h d)"), G["v0"][s].rearrange("(t p) d -> p t d", p=128), reads=[G["v0"]], writes=[Vaug], lane="V")

    def select(s, part=None):
        if part is None or part == 0:
            select_a(s)
        if part is None or part == 1:
            select_b(s)
        if part is None or part == 2:
            select_c(s)

    def select_a(s):
        kT = kTs[s % 2]
        k.op("dve", lambda e: e.tensor_reduce(out=km[:, :, :], in_=kT[:, :, :].rearrange("p c (n l) -> p c n l", n=8), axis=AX.X, op=ALU.add),
             [kT], [km])
        k.ts("dve", kmb[:, :, :], km[:, :, :], 1.0 / 256, None, ALU.mult, None, [km], [kmb])

    def select_b(s):
        qT = qTs[s % 2]
        pgs = (pb[5], pb[6])
        for tt_ in range(16):
            pg = pgs[tt_ // 8]
            o_ = (tt_ % 8) * 64
            for h in range(8):
                k.mm(pg[:, o_ + h * 8:o_ + (h + 1) * 8], qT[:, h // 2, tt_ * 128:(tt_ + 1) * 128], kmb[:, h, :], True, True, [qT, kmb], [pg])
        G0, G1, G2 = Gs
        fl = lambda t_: t_[:, :, :].rearrange("p a b -> p (a b)")
        mb = bview(m_.ap, [[1, 128], [0, 8]])
        for hf in range(2):
            k.tt("dve", fl(G0)[:, hf * 512:(hf + 1) * 512], pgs[hf][:, :], fl(negm)[:, hf * 512:(hf + 1) * 512], ALU.add, [pgs[hf], negm], [G0])
        cur = G0
        for nxt in (G1, G2):
            k.op("dve", lambda e, cur=cur: e.tensor_reduce(out=m_[:, :], in_=cur[:, :, :], axis=AX.X, op=ALU.max), [cur], [m_])
            k.tt("dve", tmask[:, :, :], cur[:, :, :], mb, ALU.is_ge, [cur, m_], [tmask])
            k.stt(fl(nxt), fl(tmask), -1e30, fl(cur), ALU.mult, ALU.add, [tmask, cur], [nxt])
            cur = nxt
        k.op("dve", lambda e, cur=cur: e.tensor_reduce(out=m_[:, :], in_=cur[:, :, :], axis=AX.X, op=ALU.max), [cur], [m_])
        k.tt("dve", tmask[:, :, :], G0[:, :, :], mb, ALU.is_ge, [G0, m_], [tmask])
        k.tt("dve", fl(tmask), fl(tmask), fl(elig), ALU.mult, [tmask, elig], [tmask])
        k.tt("dve", fl(tmask), fl(tmask), fl(ownm), ALU.add, [tmask, ownm], [tmask])
        k.ts("dve", bq[:, :, 0:64], tmask[:, :, :].rearrange("p (t a) b -> p t (a b)", t=16), -1.0, -NEG, ALU.add, ALU.mult, [tmask], [bq])

    def select_c(s):
        biasT = biasTs[s % 2]
        for tt_ in range(16):
            pt_ = pb[7]
            ptv = pt_.ap.bitcast(BF16)
            k.tr(ptv[:, 0:128], bq[:, tt_, :], C["idb"][:, :], [bq, C["idb"]], [pt_])
            k.copy("act", biasT[:, tt_ * 128:(tt_ + 1) * 128], ptv[0:64, 0:128], [pt_], [biasT])

    load_kq(0)
    load_v(0)
    select(0)
    for s in range(NSEQ):
        kT, qT, biasT = kTs[s % 2], qTs[s % 2], biasTs[s % 2]
        heads = []
        for h in range(8):
            cc = h // 2
            heads.append({"kT": kT, "qT": qT, "Va": Vaug, "omix": omix, "ocol": h * 64, "bias": (Z, biasT, h),
                          "k": (lambda kt, h=h, kT=kT: kT[:, h, kt * 128:(kt + 1) * 128]),
                          "q": (lambda c, j0, cc=cc, qT=qT: qT[:, cc, c * 512 + j0:(c + 1) * 512]),
                          "v": (lambda kt, h=h: Vaug[:, kt, h, :])})
        if s + 1 < NSEQ:
            heads[0]["after"] = (lambda s=s: load_kq(s + 1))
            heads[2]["after"] = (lambda s=s: select(s + 1, 0))
            heads[4]["after"] = (lambda s=s: select(s + 1, 1))
            heads[6]["after"] = (lambda s=s: select(s + 1, 2))
        attention_run(k, C, heads, 0.125, cmask, Pt)
        k.dma("sp", G["mix0"][s, :, 0:512].rearrange("(t p) d -> p t d", p=128), omix[:, :, :], reads=[omix], writes=[(G["mix0"], (s, "a"))], lane="om")
        if s + 1 < NSEQ:
            load_v(s + 1)


def phase_C0(k, C, G):
    k.phase("C0")
    pb = k.banks
    dtb = k.sb("dtb", [128, 8]); Ab = k.sb("Ab", [128, 8]); dsk = k.sb("dsk", [128, 8]); gn = k.sb("gn", [128, 512])
    for t_, nm, ln, w in ((dtb, "dtb", "c1", 8), (Ab, "alog", "c2", 8), (dsk, "dskip", "c3", 8), (gn, "ssdg", "c4", 512)):
        k.dma("sp", t_[:, :], G[nm][:, :].broadcast_to([128, w]), writes=[t_], lane=ln)
    triLE = k.sb("triLE", [128, 128]); strict = k.sb("strict", [128, 128]); tribias = k.sb("tribias", [128, 512], BF16); onesF = k.sb("onesF", [128, 128])
    k.dma("sp", triLE[:, :], G["triLE"][:, :], writes=[triLE], lane="c5")
    k.dma("sp", strict[:, :], G["strict"][:, :], writes=[strict], lane="c1")
    k.dma("pool", tribias[:, :], G["tribias"][:, :], writes=[tribias], lane="c2")
    k.dma("sp", onesF[:, :], G["onesF"][:, :], writes=[onesF], lane="c3")
    k.act(Ab[:, :], Ab[:, :], AF.Exp, [Ab], [Ab])
    k.ts("dve", Ab[:, :], Ab[:, :], -1.0, None, ALU.mult, None, [Ab], [Ab])
    C["junk"] = k.sb("junk", [128, 1024], BF16)
    ob = outproj_alloc(k, G, "w_out0")
    v3 = lambda t_: t_[:, :].rearrange("p (h d) -> p h d", h=8)

    def mk_src(am, ym):
        return lambda tt, c: (am[:, tt, c * 128:(c + 1) * 128], am) if c < 4 else (ym[:, tt, (c - 4) * 128:(c - 3) * 128], ym)

    ctx = []
    for s in range(NSEQ):
        T_ = {}
        T_["xb"] = k.sb(f"xb{s}", [128, 8, 1024], BF16); T_["zs"] = k.sb(f"zs{s}", [128, 8, 512], BF16)
        T_["dt"] = k.sb(f"dt{s}", [128, 16, 8]); T_["a_"] = k.sb(f"a{s}", [128, 16, 8])
        T_["xtok"] = k.sb(f"xtok{s}", [128, 512]); T_["Btok"] = k.sb(f"Btok{s}", [128, 256], BF16)
        T_["xd"] = k.sb(f"xd{s}", [128, 8, 64], BF16); T_["xdd"] = k.sb(f"xdd{s}", [128, 8, 64], BF16)
        T_["cs"] = k.sb(f"cs{s}", [128, 16]); T_["eacs"] = k.sb(f"eacs{s}", [128, 8]); T_["dte"] = k.sb(f"dte{s}", [128, 8]); T_["etot"] = k.sb(f"etot{s}", [128, 8])
        T_["amask"] = k.sb(f"amask{s}", [128, 8, 128])
        T_["Lt"] = k.sb(f"Lt{s}", [128, 8, 128]); T_["Mt"] = k.sb(f"Mt{s}", [128, 8, 128], BF16)
        T_["H"] = k.sb(f"H{s}", [128, 8, 64]); T_["Hb"] = k.sb(f"Hb{s}", [128, 8, 64], BF16)
        T_["y"] = k.sb(f"y{s}", [128, 512]); T_["tmp"] = k.sb(f"tmp{s}", [128, 512]); T_["tmp2"] = k.sb(f"tmp2{s}", [128, 512])
        T_["ssq"] = k.sb(f"ssq{s}", [128, 2]); T_["rsq"] = k.sb(f"rsq{s}", [128, 2])
        T_["ymix"] = k.sb(f"ymix{s}", [128, 16, 512], BF16); T_["amix"] = k.sb(f"amix{s}", [128, 16, 512], BF16)
        ctx.append(T_)

    def load_half(s, hf):
        T_ = ctx[s]
        k.dma("sp", T_["xb"][:, :, :], G["xbcT"][s, :, hf * 1024:(hf + 1) * 1024].rearrange("(c p) t -> p c t", p=128), reads=[G["xbcT"]], writes=[T_["xb"]], lane=f"xb{s}")
        k.dma("sp", T_["zs"][:, :, :], G["z0"][s, hf * 1024:(hf + 1) * 1024, :].rearrange("(t p) d -> p t d", p=128), reads=[G["z0"]], writes=[T_["zs"]], lane=f"zs{s}")

    for s in range(NSEQ):
        T_ = ctx[s]
        dt, a_, H, Hb, amix = T_["dt"], T_["a_"], T_["H"], T_["Hb"], T_["amix"]
        load_half(s, 0)
        k.dma("sp", amix[:, :, :], G["mix0"][s, :, 0:512].rearrange("(t p) d -> p t d", p=128), reads=[G["mix0"]], writes=[amix], lane=f"am{s}")
        k.dma("sp", dt[:, :, :], G["dt0"][s], reads=[G["dt0"]], writes=[dt], lane=f"dtl{s}")
        k.tt("dve", dt[:, :, :], dt[:, :, :], bview(dtb.ap, [[0, 16], [1, 8]]), ALU.add, [dt, dtb], [dt])
        k.act(a_[:, :, :], dt[:, :, :], AF.Abs, [dt], [a_])
        k.act(a_[:, :, :], a_[:, :, :], AF.Exp, [a_], [a_], scale=-1.0)
        k.act(a_[:, :, :], a_[:, :, :], AF.Ln, [a_, C["one"]], [a_], bias=C["one"][:, :])
        k.ts("dve", dt[:, :, :], dt[:, :, :], 0.0, None, ALU.max, None, [dt], [dt])
        k.tt("dve", dt[:, :, :], dt[:, :, :], a_[:, :, :], ALU.add, [dt, a_], [dt])
        k.tt("dve", a_[:, :, :], dt[:, :, :], bview(Ab.ap, [[0, 16], [1, 8]]), ALU.mult, [dt, Ab], [a_])
        k.memset("dve", H[:, :, :], 0.0, [H])
        k.memset("pool", Hb[:, :, :], 0.0, [Hb])

    def body(s, ch):
        T_ = ctx[s]
        xb, zs, dt, a_, xtok, Btok, xd, xdd = T_["xb"], T_["zs"], T_["dt"], T_["a_"], T_["xtok"], T_["Btok"], T_["xd"], T_["xdd"]
        cs, eacs, dte, etot, amask, Lt, Mt = T_["cs"], T_["eacs"], T_["dte"], T_["etot"], T_["amask"], T_["Lt"], T_["Mt"]
        H, Hb, y, tmp, tmp2, ssq, rsq, ymix = T_["H"], T_["Hb"], T_["y"], T_["tmp"], T_["tmp2"], T_["ssq"], T_["rsq"], T_["ymix"]
        cols = slice((ch % 8) * 128, (ch % 8 + 1) * 128)
        X = (pb[0], pb[1], pb[2]) if s == 0 else (pb[3], pb[4], pb[5])
        ptr = X[0]
        pv = ptr.ap.bitcast(BF16)
        for j in range(6):
            k.tr(pv[:, j * 128:(j + 1) * 128], xb[:, j, cols], C["idb"][:, :], [xb, C["idb"]], [ptr])
        k.copy("act", xtok[:, :], pv[:, 0:512], [ptr], [xtok])
        k.copy("dve", Btok[:, :], pv[:, 512:768], [ptr], [Btok])
        dtc = bview(dt[:, ch, :], [[1, 8], [0, 64]])
        k.tt("dve", xd[:, :, :], v3(xtok), dtc, ALU.mult, [xtok, dt], [xd])
        yield
        pc = X[1]
        k.mm(pc[:, 0:8], triLE[:, :], a_[:, ch, :], True, True, [triLE, a_], [pc])
        k.mm(pc[:, 8:16], onesF[:, :], a_[:, ch, :], True, True, [onesF, a_], [pc])
        k.copy("dve", cs[:, :], pc[:, 0:16], [pc], [cs])
        k.act(eacs[:, :], cs[:, 0:8], AF.Exp, [cs], [eacs])
        k.act(etot[:, :], cs[:, 8:16], AF.Exp, [cs], [etot])
        k.tt("dve", dte[:, :], cs[:, 8:16], cs[:, 0:8], ALU.subtract, [cs], [dte])
        k.act(dte[:, :], dte[:, :], AF.Exp, [dte], [dte])
        k.tt("pool", xdd[:, :, :], xd[:, :, :], bview(dte.ap, [[1, 8], [0, 64]]), ALU.mult, [xd, dte], [xdd])
        k.tt("dve", amask[:, :, :], bview(strict.ap, [[0, 8], [1, 128]]), bview(a_[:, ch, :], [[1, 8], [0, 128]]), ALU.mult,
             [strict, a_], [amask])
        yield
        pcb = X[2]
        for g in range(2):
            k.mm(pcb[:, g * 128:(g + 1) * 128], xb[:, 4 + g, cols], xb[:, 6 + g, cols], True, True, [xb], [pcb])
        for g in range(2):
            pL = X[g]
            k.mm(pL[:, :], C["idb"][:, :], tribias[:, :], True, False, [C["idb"], tribias], [pL], sgc=True)
            for hh in range(4):
                k.mm(pL[:, hh * 128:(hh + 1) * 128], amask[:, 4 * g + hh, :], triLE[:, :], False, True, [amask, triLE], [pL], sgc=True)
            k.act(Lt[:, 4 * g:4 * g + 4, :], pL[:, :].rearrange("p (h l) -> p h l", h=4), AF.Exp, [pL], [(Lt, g)])
            k.tt("dve", Mt[:, 4 * g:4 * g + 4, :], Lt[:, 4 * g:4 * g + 4, :], bview(pcb[:, g * 128:(g + 1) * 128], [[0, 4], [1, 128]]), ALU.mult,
                 [(Lt, g), pcb], [(Mt, g)])
        yield
        pY, pYo, pS = X[0], X[1], X[2]
        for h in range(8):
            g = h // 4
            k.mm(pY[:, h * 64:(h + 1) * 64], Mt[:, h, :], xd[:, h, :], True, True, [(Mt, g), xd], [pY])
        for h in range(8):
            g = h // 4
            k.mm(pYo[:, h * 64:(h + 1) * 64], xb[:, 6 + g, cols], Hb[:, h, :], True, True, [xb, Hb], [pYo])
        for h in range(8):
            g = h // 4
            k.mm(pS[:, h * 64:(h + 1) * 64], Btok[:, g * 128:(g + 1) * 128], xdd[:, h, :], True, True, [Btok, xdd], [pS])
        k.tt("dve", v3(tmp), pYo[:, :].rearrange("p (h d) -> p h d", h=8), bview(eacs.ap, [[1, 8], [0, 64]]), ALU.mult, [pYo, eacs], [tmp])
        k.tt("dve", y[:, :], tmp[:, :], pY[:, :], ALU.add, [tmp, pY], [y])
        k.tt("pool", v3(tmp2), v3(xtok), bview(dsk.ap, [[1, 8], [0, 64]]), ALU.mult, [xtok, dsk], [tmp2])
        k.tt("pool", y[:, :], y[:, :], tmp2[:, :], ALU.add, [y, tmp2], [y])
        k.tt("pool", y[:, :], y[:, :], zs[:, ch % 8, :], ALU.mult, [y, (zs, ch % 8)], [y])
        k.tt("dve", H[:, :, :], H[:, :, :], bview(etot.ap, [[1, 8], [0, 64]]), ALU.mult, [H, etot], [H])
        k.tt("dve", H[:, :, :], H[:, :, :], pS[:, :].rearrange("p (h d) -> p h d", h=8), ALU.add, [H, pS], [H])
        k.copy("act", Hb[:, :, :], H[:, :, :], [H], [Hb])
        yield
        for g in range(2):
            k.act(C["junk"][:, 0:256], y[:, g * 256:(g + 1) * 256], AF.Square, [y], [C["junk"], (ssq, g)], accum_out=ssq[:, g:g + 1])
        k.act(rsq[:, :], ssq[:, :], AF.Ln, [ssq, C["eps"]], [rsq], scale=1.0 / 256, bias=C["eps"][:, :])
        k.act(rsq[:, :], rsq[:, :], AF.Exp, [rsq], [rsq], scale=-0.5)
        k.tt("dve", tmp[:, :].rearrange("p (g d) -> p g d", g=2), y[:, :].rearrange("p (g d) -> p g d", g=2),
             bview(rsq.ap, [[1, 2], [0, 256]]), ALU.mult, [y, rsq], [tmp])
        k.tt("dve", ymix[:, ch, :], tmp[:, :], gn[:, :], ALU.mult, [tmp, gn], [(ymix, ch)])
        yield

    for ch in range(16):
        if ch == 8:
            for s in range(NSEQ):
                load_half(s, 1)
        gens = [body(s, ch) for s in range(NSEQ)]
        live = list(gens)
        if os.environ.get("C0_SEQ"):
            for g_ in gens:
                for _ in g_:
                    pass
            live = []
        while live:
            for g_ in list(live):
                try:
                    next(g_)
                except StopIteration:
                    live.remove(g_)
    for s in range(NSEQ):
        outproj_seq(k, C, s, mk_src(ctx[s]["amix"], ctx[s]["ymix"]), ob, G["x"], G["x1"])


def outproj_alloc(k, G, wname):
    Wo = k.sb("Wo", [128, 8, D], BF16)
    load_w(k, Wo, G[wname], 8, "wo")
    xt = [k.sb(f"xt{i}", [128, D]) for i in range(2)]
    aT = [k.sb(f"aT{i}", [128, 8, 128], BF16) for i in range(2)]
    return Wo, xt, aT


def outproj_seq(k, C, s, src, ob, xin, xout, tiles=range(16), banks=(7, 5, 6)):
    Wo, xt, aT = ob
    pb = k.banks
    for tt in tiles:
        b = tt % 2
        rows = slice(tt * 128, (tt + 1) * 128)
        k.dma("sp", xt[b][:, :], xin[s, rows, :], reads=[xin], writes=[xt[b]], lane=f"xt{b}")
        ptr = pb[banks[0]]
        pv = ptr.ap.bitcast(BF16)
        for c in range(8):
            ap, tens = src(tt, c)
            k.tr(pv[:, c * 128:(c + 1) * 128], ap, C["idb"][:, :], [tens, C["idb"]], [ptr])
        k.copy("act", aT[b][:, :, :], pv.rearrange("p (c n) -> p c n", c=8), [ptr], [aT[b]])
        for hf in range(2):
            po = pb[banks[1 + hf]]
            for c in range(8):
                k.mm(po[:, :], aT[b][:, c, :], Wo[:, c, hf * 512:(hf + 1) * 512], c == 0, c == 7, [aT[b], Wo], [po])
            k.tt("dve", xt[b][:, hf * 512:(hf + 1) * 512], xt[b][:, hf * 512:(hf + 1) * 512], po[:, :], ALU.add, [xt[b], po], [xt[b]])
        k.dma("sp", xout[s, rows, :], xt[b][:, :], reads=[xt[b]], writes=[(xout, (s, tt))], lane=f"xs{b}")


def phase_D(k, C, G, l, xin, xout):
    k.phase(f"D{l}")
    pb = k.banks
    TC = 512
    NT = TC // 128
    Wup = k.sb("Wup", [128, 8, 2 * DFF], BF16)
    jbounds = [0, 2, 8, 15, 22]
    wsrc = G[f"wup{l}"][:, :].rearrange("(c p) n -> p c n", p=128)
    for bi in range(4):
        j0, j1 = jbounds[bi], jbounds[bi + 1]
        for hf in range(2):
            c0, c1 = hf * DFF + j0 * 128, hf * DFF + j1 * 128
            k.dma("pool", Wup[:, :, c0:c1], wsrc[:, :, c0:c1], writes=[(Wup, (bi, hf))], lane=f"wu{bi}{hf}")
    ublk = lambda jj: ([bi for bi in range(4) if jbounds[bi] <= (jj % 22) < jbounds[bi + 1]][0], jj // 22)
    Wdn = k.sb("Wdn", [128, 22, D], BF16)
    for bi in range(2):
        k.dma("pool", Wdn[:, bi * 11:(bi + 1) * 11, :], G[f"wdn{l}"][bi * 1408:(bi + 1) * 1408, :].rearrange("(c p) n -> p c n", p=128),
              writes=[Wdn], lane="wd", group=(bi > 0))
    gb = k.sb("gb", [128, D])
    k.dma("sp", gb[:, :], G[f"ffng{l}"][:, :].broadcast_to([128, D]), writes=[gb], lane="c1")
    fcw = k.sb("fcw", [128, 44, 3]); fcb = k.sb("fcb", [128, 44])
    k.dma("sp", fcw[:, :, :], G[f"fcw{l}"][:, :, :], writes=[fcw], lane="c2")
    k.dma("sp", fcb[:, :], G[f"fcb{l}"][:, :], writes=[fcb], lane="c3")
    xtl = [k.sb(f"xt{i}", [128, D]) for i in range(3)]
    actT = [k.sb(f"actT{i}", [128, 8, TC], BF16) for i in range(2)]
    hid = k.sb("hid", [128, 22, TC], BF16)
    raws = [k.sb(f"raw{i}", [128, TC + 2]) for i in range(4)]
    halo = k.sb("halo", [128, 44, 2])
    accg = [k.sb(f"accg{i}", [128, TC]) for i in range(2)]
    accu = [k.sb(f"accu{i}", [128, TC]) for i in range(2)]
    nchunk = NSEQ * S // TC
    xc = [0]
    rc = [0]

    def rows_of(i, t):
        s, pos = divmod(i * TC, S)
        return s, pos + t * 128

    def norm_tile(i, t):
        s, r0 = rows_of(i, t)
        x_ = xtl[xc[0] % 3]
        xc[0] += 1
        hn = C["hn"][t % 2]
        k.dma("sp", x_[:, :], xin[s, r0:r0 + 128, :], reads=[xin], writes=[x_], lane=f"xl{xc[0] % 3}")
        k.act(hn[:, :], x_[:, :], AF.Square, [x_], [hn, (C["ss"], t)], accum_out=C["ss"][:, t:t + 1])
        k.act(C["rs"][:, t:t + 1], C["ss"][:, t:t + 1], AF.Ln, [(C["ss"], t), C["eps"]], [(C["rs"], t)], scale=1.0 / D, bias=C["eps"][:, :])
        k.act(C["rs"][:, t:t + 1], C["rs"][:, t:t + 1], AF.Exp, [(C["rs"], t)], [(C["rs"], t)], scale=-0.5)
        k.stt(hn[:, :], x_[:, :], C["rs"][:, t:t + 1], gb[:, :], ALU.mult, ALU.mult, [x_, (C["rs"], t), gb], [hn])

    def tr_tile(i, t):
        hn = C["hn"][t % 2]
        aT = actT[i % 2]
        ptr = pb[7]
        pv = ptr.ap.bitcast(BF16)
        for c in range(8):
            k.tr(pv[:, c * 128:(c + 1) * 128], hn[:, c * 128:(c + 1) * 128], C["idb"][:, :], [hn, C["idb"]], [ptr])
        k.copy("act", aT[:, :, t * 128:(t + 1) * 128], pv.rearrange("p (c n) -> p c n", c=8), [ptr], [(aT, t)])

    def up(i):
        s, pos = divmod(i * TC, S)
        aT = actT[i % 2]

        def tail(j):
            b = j % 2
            k.act(accg[b][:, :], accg[b][:, :], AF.Silu, [accg[b]], [accg[b]])
            k.tt("dve", hid[:, j, :], accg[b][:, :], accu[b][:, :], ALU.mult, [accg[b], accu[b]], [(hid, j)])

        for j in range(22):
            b = j % 2
            raws_j = []
            for half, jj, acc in ((0, j, accg[b]), (1, 22 + j, accu[b])):
                pu = pb[2 + (2 * j + half) % 4]
                for c in range(8):
                    k.mm(pu[:, 0:TC], Wup[:, c, jj * 128:(jj + 1) * 128], aT[:, c, :], c == 0, c == 7, [(Wup, ublk(jj)), aT], [pu])
                raw = raws[rc[0] % 4]
                rc[0] += 1
                raws_j.append(raw)
                if pos == 0:
                    k.memset("pool", raw[:, 0:2], 0.0, [raw])
                else:
                    k.copy("act", raw[:, 0:2], halo[:, jj, :], [(halo, jj)], [raw])
                k.copy("act", raw[:, 2:TC + 2], pu[:, 0:TC], [pu], [raw])
                k.copy("act", halo[:, jj, :], raw[:, TC:TC + 2], [raw], [(halo, jj)])
                k.act(acc[:, :], pu[:, 0:TC], AF.Identity, [pu, fcw, fcb], [acc], scale=fcw[:, jj, 2:3], bias=fcb[:, jj:jj + 1])
            if j > 0:
                tail(j - 1)
            for (half, jj, acc), raw in zip(((0, j, accg[b]), (1, 22 + j, accu[b])), raws_j):
                for q_ in range(0, 2):
                    k.stt(acc[:, :], raw[:, q_:q_ + TC], fcw[:, jj, q_:q_ + 1], acc[:, :], ALU.mult, ALU.add, [raw, fcw, acc], [acc])
        tail(21)

    def down_tile(i, t):
        s, r0 = rows_of(i, t)
        x_ = xtl[xc[0] % 3]
        xc[0] += 1
        k.dma("sp", x_[:, :], xin[s, r0:r0 + 128, :], reads=[xin], writes=[x_], lane=f"xl{xc[0] % 3}")
        for hf in range(2):
            po = (pb[0], pb[1], pb[6])[(2 * t + hf) % 3]
            for j in range(22):
                k.mm(po[:, :], hid[:, j, t * 128:(t + 1) * 128], Wdn[:, j, hf * 512:(hf + 1) * 512], j == 0, j == 21, [(hid, j), Wdn], [po])
            k.tt("dve", x_[:, hf * 512:(hf + 1) * 512], x_[:, hf * 512:(hf + 1) * 512], po[:, :], ALU.add, [x_, po], [x_])
        k.dma("sp", xout[s, r0:r0 + 128, :], x_[:, :], reads=[x_], writes=[(xout, (s, r0))], lane=f"xst{xc[0] % 3}")

    for t in range(NT):
        norm_tile(0, t)
        tr_tile(0, t)
    for i in range(nchunk):
        up(i)
        for t in range(NT):
            if i + 1 < nchunk:
                norm_tile(i + 1, t)
            down_tile(i, t)
            if i + 1 < nchunk:
                tr_tile(i + 1, t)


def phase_A1(k, C, G):
    k.phase("A1")
    pb = k.banks
    Win = k.sb("Win", [128, 8, 800], BF16)
    load_w(k, Win, G["w_in1"], 8, "w0")
    Wuq = k.sb("Wuq", [128, 4, 1536], BF16)
    load_w(k, Wuq, G["w_uq"], 4, "w1")
    Wkv = k.sb("Wkv", [128, 2, 2048], BF16)
    load_w(k, Wkv, G["w_ukv"], 2, "w2")
    WqN = k.sb("WqN", [128, 4, 1024], BF16); WqR = k.sb("WqR", [128, 4, 512], BF16); WkN = k.sb("WkN", [128, 2, 1024], BF16)
    for c in range(4):
        wv_ = Wuq[:, c, :].rearrange("p (h d) -> p h d", h=16)
        k.copy("pool", WqN[:, c, :].rearrange("p (h d) -> p h d", h=16), wv_[:, :, 0:64], [Wuq], [WqN])
        k.copy("pool", WqR[:, c, :].rearrange("p (h d) -> p h d", h=16), wv_[:, :, 64:96], [Wuq], [WqR])
    for c in range(2):
        k.copy("pool", WkN[:, c, :].rearrange("p (h d) -> p h d", h=16), Wkv[:, c, :].rearrange("p (h d) -> p h d", h=16)[:, :, 0:64], [Wkv], [WkN])
    gb = k.sb("gb", [128, D])
    k.dma("sp", gb[:, :], G["mixg1"][:, :].broadcast_to([128, D]), writes=[gb], lane="c1")
    cosT = k.sb("cosT", [128, S]); sinT = k.sb("sinT", [128, S])
    k.dma("sp", cosT[:, :], G["cosR"][:, :], writes=[cosT], lane="c2")
    k.dma("sp", sinT[:, :], G["sinR"][:, :], writes=[sinT], lane="c3")
    cm = {}
    for i_, nm in enumerate(("R32", "bones", "bones32", "indRN0", "indRN1", "indNR0", "indNR1", "rep32", "onesF")):
        cm[nm] = k.sb(nm, [128, 128], BF16)
        k.dma("pool", cm[nm][:, :], G[nm][:, :], writes=[cm[nm]], lane=f"c{4 + i_ % 2}")
    onesb = cm["onesF"]
    gqa = k.sb("gqa", [128, 4]); gkva = k.sb("gkva", [128, 2])
    k.dma("sp", gqa[:, :], G["gqa"][:, :], writes=[gqa], lane="c1")
    k.dma("sp", gkva[:, :], G["gkva"][:, :], writes=[gkva], lane="c2")
    gv = {}
    for i_, nm in enumerate(("gqN", "gqR", "gkN", "gkR")):
        gv[nm] = k.sb(nm, [128, 1])
        k.dma("sp", gv[nm][:, :], G[nm][:, :], writes=[gv[nm]], lane=f"c{1 + i_ % 3}")
    xs2 = [k.sb(f"xs{i}", [128, 4, D]) for i in range(2)]
    hnT2 = [k.sb(f"hnT{i}", [128, 8, 512], BF16) for i in range(2)]
    hn4 = C["hn"] + [k.sb(f"hnx{i}", [128, 1024], BF16) for i in range(2)]
    craw = k.sb("craw", [128, 6, 512]); csq = k.sb("csq", [128, 6, 512], BF16)
    crs = k.sb("crs", [128, 2, 512])
    cn = k.sb("cn", [128, 6, 512], BF16)
    kpe = k.sb("kpe", [128, 512], BF16)
    k.memset("pool", kpe[:, :], 0.0, [kpe])
    W3 = lambda n, dt=F32: [[k.sb(f"{n}{p_}{i}", [128, 512], dt) for i in range(3)] for p_ in range(2)]
    sq = W3("sq", BF16); rstd = W3("rstd"); qn = W3("qn", BF16)
    t1 = [k.sb(f"t1{i}", [128, 512], BF16) for i in range(2)]
    t2 = [k.sb(f"t2{i}", [128, 512], BF16) for i in range(2)]
    qoR = [k.sb(f"qoR{i}", [128, 512], BF16) for i in range(2)]
    vo = [k.sb(f"vo{i}", [128, 16, 65], BF16) for i in range(2)]
    for b_ in range(2):
        k.memset("pool", vo[b_][:, :, 64:65], 1.0, [vo[b_]])
    gcount = [0]
    NCH = 2 * NSEQ * 2

    def xload(i):
        s, pos = i // 4, (i % 4) * 512
        k.dma("sp", xs2[i % 2][:, :, :], G["x2"][s, pos:pos + 512, :].rearrange("(t p) d -> p t d", p=128), reads=[G["x2"]], writes=[xs2[i % 2]], lane=f"xa{i % 2}")

    xload(0)
    for t in range(4):
        norm_stats(k, C, xs2[0], t, gb, hn4[t])
        norm_tr(k, C, hn4[t], hnT2[0], t)
    for i in range(NCH):
        s, pos = i // 4, (i % 4) * 512
        hnT = hnT2[i % 2]
        for cc in range(6):
            acc_b = pb[cc % 2]
            for c in range(8):
                k.mm(acc_b[:, :], Win[:, c, cc * 128:(cc + 1) * 128], hnT[:, c, :], c == 0, c == 7, [Win, hnT], [acc_b])
            k.copy("dve", craw[:, cc, :], acc_b[:, :], [acc_b], [(craw, cc)])
            k.act(csq[:, cc, :], acc_b[:, :], AF.Square, [acc_b], [(csq, cc)])
        pk = pb[7]
        for c in range(8):
            k.mm(pk[0:32, :], Win[:, c, 768:800], hnT[:, c, :], c == 0, c == 7, [Win, hnT], [pk])
        k.copy("act", kpe[0:32, :], pk[0:32, :], [pk], [kpe])
        for grp, (c0, c1, gg, nf) in enumerate(((0, 4, gqa, 512.0), (4, 6, gkva, 256.0))):
            pss = pb[5 + grp]
            for cc in range(c0, c1):
                k.mm(pss[:, :], onesb[:, :], csq[:, cc, :], cc == c0, cc == c1 - 1, [onesb, (csq, cc)], [pss])
            k.act(crs[:, grp, :], pss[:, :], AF.Ln, [pss, C["eps"]], [(crs, grp)], scale=1.0 / nf, bias=C["eps"][:, :])
            k.act(crs[:, grp, :], crs[:, grp, :], AF.Exp, [(crs, grp)], [(crs, grp)], scale=-0.5)
            for cc in range(c0, c1):
                k.stt(cn[:, cc, :], craw[:, cc, :], gg[:, cc - c0:cc - c0 + 1], crs[:, grp, :], ALU.mult, ALU.mult, [(craw, cc), gg, (crs, grp)], [(cn, cc)])
        gbase = gcount[0]
        gcount[0] += 8

        def stA(gi, gbase=gbase):
            par = (gbase + gi) % 2
            kind, g4 = gi // 4, gi % 4
            banks = pb[0:3] if par == 0 else pb[3:6]
            for ti in range(3):
                bank = banks[ti]
                if kind == 0:
                    if ti < 2:
                        c0_ = (4 * g4 + 2 * ti) * 64
                        for c in range(4):
                            k.mm(bank[:, :], WqN[:, c, c0_:c0_ + 128], cn[:, c, :], c == 0, c == 3, [WqN, cn], [bank])
                    else:
                        c0_ = 4 * g4 * 32
                        for c in range(4):
                            k.mm(bank[:, :], WqR[:, c, c0_:c0_ + 128], cn[:, c, :], c == 0, c == 3, [WqR, cn], [bank])
                else:
                    if ti < 2:
                        c0_ = (4 * g4 + 2 * ti) * 64
                        for c in range(2):
                            k.mm(bank[:, :], WkN[:, c, c0_:c0_ + 128], cn[:, 4 + c, :], c == 0, c == 1, [WkN, cn], [bank])
                    else:
                        k.mm(bank[:, :], cm["rep32"][:, :], kpe[:, :], True, True, [cm["rep32"], kpe], [bank])
                k.act(sq[par][ti][:, :], bank[:, :], AF.Square, [bank], [sq[par][ti]])

        def stB(gi, s=s, pos=pos, gbase=gbase):
            par = (gbase + gi) % 2
            kind, g4 = gi // 4, gi % 4
            banks = pb[0:3] if par == 0 else pb[3:6]
            gN, gR = (gv["gqN"], gv["gqR"]) if kind == 0 else (gv["gkN"], gv["gkR"])
            dst = G["qT1"] if kind == 0 else G["kT1"]
            sqs = sq[par]
            contrib = (
                ((cm["bones"], sqs[0]), (cm["indRN0"], sqs[2])),
                ((cm["bones"], sqs[1]), (cm["indRN1"], sqs[2])),
                ((cm["bones32"], sqs[2]), (cm["indNR0"], sqs[0]), (cm["indNR1"], sqs[1])),
            )
            for ti in range(3):
                pss = pb[6]
                lst = contrib[ti]
                for ci, (lh, sqt) in enumerate(lst):
                    k.mm(pss[:, :], lh[:, :], sqt[:, :], ci == 0, ci == len(lst) - 1, [lh, sqt], [pss])
                rs_ = rstd[par][ti]
                k.act(rs_[:, :], pss[:, :], AF.Ln, [pss, C["eps"]], [rs_], scale=1.0 / 96, bias=C["eps"][:, :])
                k.act(rs_[:, :], rs_[:, :], AF.Exp, [rs_], [rs_], scale=-0.5)
                g_ = gN if ti < 2 else gR
                k.stt(qn[par][ti][:, :], banks[ti][:, :], g_[:, 0:1], rs_[:, :], ALU.mult, ALU.mult, [banks[ti], g_, rs_], [qn[par][ti]])
            for ti in range(2):
                for hn_ in range(2):
                    h = 4 * g4 + 2 * ti + hn_
                    k.dma("sp", dst[s, h, 0:64, pos:pos + 512], qn[par][ti][hn_ * 64:(hn_ + 1) * 64, :], reads=[qn[par][ti]],
                          writes=[(dst, (s, h, 0, pos))], lane=f"qa{par}{2 * ti + hn_}")

        def stC(gi, s=s, pos=pos, gbase=gbase):
            par = (gbase + gi) % 2
            kind, g4 = gi // 4, gi % 4
            dst = G["qT1"] if kind == 0 else G["kT1"]
            prx = pb[7]
            qr = qn[par][2]
            k.mm(prx[:, :], cm["R32"][:, :], qr[:, :], True, True, [cm["R32"], qr], [prx])
            k.tt("pool", t1[par][:, :], qr[:, :], cosT[:, pos:pos + 512], ALU.mult, [qr, cosT], [t1[par]])
            k.tt("dve", t2[par][:, :], prx[:, :], sinT[:, pos:pos + 512], ALU.mult, [prx, sinT], [t2[par]])
            k.tt("dve", qoR[par][:, :], t1[par][:, :], t2[par][:, :], ALU.add, [t1[par], t2[par]], [qoR[par]])
            for hq in range(4):
                h = 4 * g4 + hq
                k.dma("sp", dst[s, h, 64:96, pos:pos + 512], qoR[par][hq * 32:(hq + 1) * 32, :], reads=[qoR[par]],
                      writes=[(dst, (s, h, 1, pos))], lane=f"qa{par}{4 + hq}")

        pipeline(8, [(0, stA), (1, stB), (2, stC)])
        if i + 1 < NCH:
            xload(i + 1)
            for t in range(4):
                norm_stats(k, C, xs2[(i + 1) % 2], t, gb, hn4[t])
        for t in range(4):
            b = t % 2
            for hf in range(2):
                pvv = pb[4 + hf]
                wv = lambda c, hf=hf: Wkv[:, c, :].rearrange("p (h d) -> p h d", h=16)[:, hf * 8:(hf + 1) * 8, 64:128]
                for c in range(2):
                    k.mm(pvv[:, :].rearrange("p (h d) -> p h d", h=8), cn[:, 4 + c, t * 128:(t + 1) * 128], wv(c), c == 0, c == 1, [cn, Wkv], [pvv])
                k.copy("dve" if hf == 0 else "act", vo[b][:, hf * 8:(hf + 1) * 8, 0:64], pvv[:, :].rearrange("p (h d) -> p h d", h=8), [pvv], [(vo[b], hf)])
            r0 = pos + t * 128
            k.dma("sp", G["v1"][s, r0:r0 + 128, :], vo[b][:, :, :].rearrange("p h d -> p (h d)"), reads=[vo[b]], writes=[(G["v1"], (s, r0))], lane=f"vo{b}")
        if i + 1 < NCH:
            for t in range(4):
                norm_tr(k, C, hn4[t], hnT2[(i + 1) % 2], t)


def phase_B1(k, C, G):
    k.phase("B1")
    cmask = k.sb("cmask", [128, 4, 512], BF16)
    k.dma("pool", cmask[:, :, :], G["cmask"][:, :, :], writes=[cmask], lane="c2")
    Pt = [k.sb(f"Pt{i}", [128, 512], BF16) for i in range(3)]
    omixs = [k.sb(f"omix{i}", [128, 16, 1024], BF16) for i in range(2)]
    vst = k.sb("vst", [128, 16, 16 * 65], BF16)
    C["rec"] = [k.sb(f"rec{i}", [128, 4]) for i in range(2)]
    ob = outproj_alloc(k, G, "w_out1")
    NB = 3
    kTs = [k.sb(f"kT{i}", [96, S], BF16) for i in range(NB)]
    qTs = [k.sb(f"qT{i}", [96, S], BF16) for i in range(NB)]
    n = 0
    for s in range(NSEQ):
        omix = omixs[s % 2]
        k.dma("sp", vst[:, :, :], G["v1"][s].rearrange("(t p) d -> p t d", p=128), reads=[G["v1"]], writes=[vst], lane="V")
        heads = []
        for h in range(16):
            b = n % NB
            n += 1
            kT, qT = kTs[b], qTs[b]

            def load(s=s, h=h, b=b, kT=kT, qT=qT):
                k.dma("sp", kT[:, :], G["kT1"][s, h, :, :], reads=[G["kT1"]], writes=[kT], lane=f"kT{b}")
                k.dma("sp", qT[:, :], G["qT1"][s, h, :, :], reads=[G["qT1"]], writes=[qT], lane=f"qT{b}")

            heads.append({"kT": kT, "qT": qT, "Va": vst, "omix": omix, "ocol": h * 64, "load": load,
                          "k": (lambda kt, kT=kT: kT[:, kt * 128:(kt + 1) * 128]),
                          "q": (lambda c, j0, qT=qT: qT[:, c * 512 + j0:(c + 1) * 512]),
                          "v": (lambda kt, h=h: vst[:, kt, h * 65:(h + 1) * 65])})
        heads[0]["load"]()
        heads[1]["load"]()
        for h in range(16):
            fns = []
            if h + 2 < 16:
                fns.append(heads[h + 2]["load"])
            if s > 0:
                po_, so_ = omixs[(s - 1) % 2], s - 1
                fns.append(lambda h=h, po_=po_, so_=so_: outproj_seq(
                    k, C, so_, (lambda tt, c: (po_[:, tt, c * 128:(c + 1) * 128], po_)), ob, G["x2"], G["x3"], tiles=[h]))
            heads[h]["after"] = (lambda fns=fns: [f() for f in fns])
        attention_run(k, C, heads, 96.0 ** -0.5, cmask, Pt)
        if s == NSEQ - 1:
            src = lambda tt, c, omix=omix: (omix[:, tt, c * 128:(c + 1) * 128], omix)
            outproj_seq(k, C, s, src, ob, G["x2"], G["x3"])


INTER = {
    "qT0": ([NSEQ, 512, S], BF16), "kT0": ([NSEQ, 512, S], BF16), "v0": ([NSEQ, S, 8 * 65], BF16),
    "z0": ([NSEQ, S, 512], BF16), "xbcT": ([NSEQ, 1024, S], BF16), "dt0": ([NSEQ, 128, 16, 8], F32),
    "mix0": ([NSEQ, S, D], BF16), "x1": ([NSEQ, S, D], F32), "x2": ([NSEQ, S, D], F32), "x3": ([NSEQ, S, D], F32),
    "qT1": ([NSEQ, 16, 96, S], BF16), "kT1": ([NSEQ, 16, 96, S], BF16), "v1": ([NSEQ, S, 16 * 65], BF16),
    "mix1": ([NSEQ, S, D], BF16),
}
ALL_PHASES = ("A0", "B0", "C0", "D0", "A1", "B1", "D1")


def build(phases=ALL_PHASES, kinds=None, cshapes=None, pshapes=None):
    kinds = kinds or {}
    nc = bass.Bass("TRN2", target_bir_lowering=False)
    k = KB(nc)
    G = {}

    def dram(name, shape, dt, default):
        G[name] = T(name, nc.dram_tensor(name, list(shape), dt, kind=kinds.get(name, default)).ap())

    dram("x", [NSEQ, S, D], F32, "ExternalInput")
    dram("y", [NSEQ, S, D], F32, "ExternalOutput")
    for name, (shape, dt) in INTER.items():
        dram(name, shape, dt, "Internal")
    for name, shape in cshapes.items():
        dram(name, shape, F32, "ExternalInput")
    for name, shape in pshapes.items():
        dram(name, shape, F32, "ExternalInput")
    C = common_consts(k, G)
    for ph in phases:
        if ph == "A0":
            phase_A0(k, C, G)
        elif ph == "B0":
            phase_B0(k, C, G)
        elif ph == "C0":
            phase_C0(k, C, G)
        elif ph == "D0":
            phase_D(k, C, G, 0, G["x1"], G["x2"])
        elif ph == "A1":
            phase_A1(k, C, G)
        elif ph == "B1":
            phase_B1(k, C, G)
        elif ph == "D1":
            phase_D(k, C, G, 1, G["x3"], G["y"])
    k.barrier()
    k.emit()
    return nc, k


def kernel(**inputs):
    consts = host_consts()
    params = host_params(inputs)
    nc, _ = build(cshapes={n: a.shape for n, a in consts.items()}, pshapes={n: a.shape for n, a in params.items()})
    x = np.ascontiguousarray(np.asarray(inputs["x"], dtype=np.float32))
    in_maps = []
    for i in range(8):
        m = {"x": x[2 * i:2 * i + 2]}
        m.update(consts)
        m.update(params)
        in_maps.append(m)
    res = run_bass_kernel_spmd(nc, in_maps, core_ids=list(range(8)))
    return np.concatenate([np.asarray(r["y"], dtype=np.float32) for r in res.results], axis=0)
```

```python
import math
import os
import numpy as np
import ml_dtypes
import concourse.bass as bass
import concourse.mybir as mybir
from concourse.bass_utils import run_bass_kernel_spmd

F32 = mybir.dt.float32
BF16 = mybir.dt.bfloat16
ALU = mybir.AluOpType
AF = mybir.ActivationFunctionType
AX = mybir.AxisListType

S = 2048
D = 1024
NSEQ = 2
NEG = -30000.0
EPS = 1e-6
DFF = 2816


class T:
    def __init__(self, name, ap):
        self.name = name
        self.ap = ap

    def __getitem__(self, idx):
        return self.ap[idx]


class _Op:
    __slots__ = ("eng", "fn", "deps", "is_dma", "lane", "target", "sig", "sigidx", "idx")


class _Ent:
    __slots__ = ("writer", "readers")

    def __init__(self, writer=None, readers=None):
        self.writer = writer
        self.readers = readers if readers is not None else []


class KB:
    ENGS = ("pe", "act", "dve", "pool", "sp")
    ARENA = 53200

    def __init__(self, nc):
        self.nc = nc
        self.ops = []
        self.eng_ops = {e: [] for e in self.ENGS}
        self.track = {}
        self.lanes = {}
        self.arena = nc.alloc_sbuf_tensor("arena", [128, self.ARENA], F32)
        self.off = 0
        self.base = 0
        self.pname = ""
        self.nsb = 0
        self.banks = [T(f"pb{i}", nc.alloc_psum_tensor(f"pb{i}", [128, 512], F32)[:, :]) for i in range(8)]

    def sb(self, name, shape, dt=F32, persist=False):
        n = 1
        for d in shape[1:]:
            n *= d
        words = n if dt == F32 else (n + 1) // 2
        words = (words + 7) // 8 * 8
        assert self.off + words <= self.ARENA, f"SBUF arena overflow at {name}: {self.off}+{words}"
        ap = self.arena[0:shape[0], self.off:self.off + words]
        if dt != F32:
            ap = ap.bitcast(dt)
        ap = ap[:, 0:n]
        if len(shape) == 3:
            ap = ap.rearrange("p (a b) -> p a b", a=shape[1])
        elif len(shape) == 4:
            ap = ap.rearrange("p (a b c) -> p a b c", a=shape[1], b=shape[2])
        self.off += words
        self.nsb += 1
        return T(f"{self.pname}.{name}.{self.nsb}", ap)

    def phase(self, name):
        self.barrier()
        self.pname = name
        self.off = self.base

    def persist_mark(self):
        self.base = self.off

    @staticmethod
    def _norm(acc):
        out = []
        for a in acc:
            if isinstance(a, tuple):
                out.append((a[0].name, a[1]))
            else:
                out.append((a.name, None))
        return out

    def _ents(self, buf, key):
        d = self.track.setdefault(buf, {None: _Ent()})
        if key is None:
            return list(d.values())
        if key not in d:
            base = d[None]
            d[key] = _Ent(base.writer, list(base.readers))
        return [d[key]]

    def _record(self, op, reads, writes):
        deps = set()
        reads = self._norm(reads)
        writes = self._norm(writes)
        wn = {w[0] for w in writes}
        writes = writes + [r for r in reads if r[0].startswith("pb") and r[0] not in wn]
        for buf, key in reads:
            for e in self._ents(buf, key):
                if e.writer is not None:
                    deps.add(e.writer)
        for buf, key in writes:
            for e in self._ents(buf, key):
                if e.writer is not None:
                    deps.add(e.writer)
                deps.update(e.readers)
        for buf, key in reads:
            for e in self._ents(buf, key):
                e.readers.append(op)
        for buf, key in writes:
            if key is None:
                self.track[buf] = {None: _Ent(op, [])}
            else:
                e = self._ents(buf, key)[0]
                e.writer = op
                e.readers = []
        deps.discard(op)
        return deps

    def _new(self, eng, fn):
        o = _Op()
        o.eng = eng
        o.fn = fn
        o.is_dma = False
        o.lane = None
        o.target = 0
        o.sig = False
        o.sigidx = 0
        o.idx = len(self.ops)
        o.deps = set()
        return o

    def op(self, eng, fn, reads=(), writes=()):
        o = self._new(eng, fn)
        o.deps = self._record(o, reads, writes)
        self.ops.append(o)
        self.eng_ops[eng].append(o)
        return o

    def dma(self, eng, out, in_, reads=(), writes=(), lane=None, group=False, **kw):
        o = self._new(eng, lambda e, out=out, in_=in_, kw=kw: e.dma_start(out=out, in_=in_, **kw))
        o.is_dma = True
        lane = f"{eng}_{lane}"
        ln = self.lanes.setdefault(lane, {"sem": None, "count": 0, "last": None, "gprev": None})
        o.lane = lane
        o.deps = self._record(o, reads, writes)
        if not group:
            ln["gprev"] = ln["last"]
        if ln["gprev"] is not None:
            o.deps.add(ln["gprev"])
        ln["count"] += 1
        o.target = 16 * ln["count"]
        ln["last"] = o
        self.ops.append(o)
        self.eng_ops[eng].append(o)
        return o

    def barrier(self):
        lasts = set()
        for e in self.ENGS:
            for o in reversed(self.eng_ops[e]):
                if o.fn is not None and not o.is_dma:
                    lasts.add(o)
                    break
        for ln in self.lanes.values():
            if ln["last"] is not None:
                lasts.add(ln["last"])
        if not lasts:
            return
        for e in self.ENGS:
            o = self._new(e, None)
            o.deps = set(lasts)
            self.ops.append(o)
            self.eng_ops[e].append(o)

    def act(self, out, in_, func, reads, writes, **kw):
        return self.op("act", lambda e: e.activation(out=out, in_=in_, func=func, **kw), reads, writes)

    def tt(self, eng, out, in0, in1, op, reads, writes):
        return self.op(eng, lambda e: e.tensor_tensor(out=out, in0=in0, in1=in1, op=op), reads, writes)

    def ts(self, eng, out, in0, s1, s2, op0, op1, reads, writes):
        if op1 is None:
            return self.op(eng, lambda e: e.tensor_scalar(out=out, in0=in0, scalar1=s1, scalar2=None, op0=op0), reads, writes)
        return self.op(eng, lambda e: e.tensor_scalar(out=out, in0=in0, scalar1=s1, scalar2=s2, op0=op0, op1=op1), reads, writes)

    def stt(self, out, in0, scalar, in1, op0, op1, reads, writes):
        return self.op("dve", lambda e: e.scalar_tensor_tensor(out=out, in0=in0, scalar=scalar, in1=in1, op0=op0, op1=op1), reads, writes)

    def copy(self, eng, out, in_, reads, writes):
        if eng == "act":
            return self.op("act", lambda e: e.copy(out=out, in_=in_), reads, writes)
        return self.op(eng, lambda e: e.tensor_copy(out=out, in_=in_), reads, writes)

    def memset(self, eng, ap, val, writes):
        return self.op(eng, lambda e: e.memset(ap, val), (), writes)

    def mm(self, out, lhsT, rhs, start, stop, reads, writes, sgc=False):
        return self.op("pe", lambda e: e.matmul(out, lhsT=lhsT, rhs=rhs, start=start, stop=stop, skip_group_check=sgc), reads, writes)

    def tr(self, out, in_, ident, reads, writes):
        return self.op("pe", lambda e: e.transpose(out=out, in_=in_, identity=ident), reads, writes)

    def emit(self):
        nc = self.nc
        for o in self.ops:
            best = {}
            nd = set()
            for d in o.deps:
                if d.is_dma:
                    nd.add(d)
                else:
                    if d.eng == "pe" and o.eng == "pe" and not o.is_dma and o.fn is not None:
                        continue
                    b = best.get(d.eng)
                    if b is None or d.idx > b.idx:
                        best[d.eng] = d
            nd.update(best.values())
            o.deps = nd
            for d in nd:
                if not d.is_dma:
                    assert d.fn is not None
                    d.sig = True
        for e in self.ENGS:
            c = 0
            for o in self.eng_ops[e]:
                if o.sig and not o.is_dma:
                    c += 1
                    o.sigidx = c
        sems = {e: nc.alloc_semaphore(name=f"sem_{e}") for e in self.ENGS}
        for name, ln in self.lanes.items():
            ln["sem"] = nc.alloc_semaphore(name=f"lane_{name}")
        with nc.Block() as block:
            for e in self.ENGS:
                ops = self.eng_ops[e]
                final = (e == "sp")

                def body(h, e=e, ops=ops, final=final):
                    waited = {}
                    for o in ops:
                        for d in sorted(o.deps, key=lambda d: d.idx):
                            if d.is_dma:
                                s, v = self.lanes[d.lane]["sem"], d.target
                            else:
                                s, v = sems[d.eng], d.sigidx
                            kk = s.num
                            if waited.get(kk, 0) >= v:
                                continue
                            waited[kk] = v
                            h.wait_ge(s, v)
                        if o.fn is None:
                            continue
                        inst = o.fn(h)
                        if o.is_dma:
                            inst.then_inc(self.lanes[o.lane]["sem"], 16)
                        elif o.sig:
                            inst.then_inc(sems[e], 1)
                    if final:
                        for name, ln in self.lanes.items():
                            if ln["count"]:
                                h.wait_ge(ln["sem"], 16 * ln["count"])

                dec = {"pe": block.tensor, "act": block.scalar, "dve": block.vector,
                       "pool": block.gpsimd, "sp": block.sync}[e]
                dec(body)


def pipeline(n, stages):
    maxoff = max(o for o, _ in stages)
    for t in range(n + maxoff):
        for off, fn in stages:
            i = t - off
            if 0 <= i < n:
                fn(i)


def bview(ap, pattern):
    return bass.AP(tensor=ap.tensor, offset=ap.offset, ap=[list(ap.ap[0])] + [list(p) for p in pattern])


def host_consts():
    c = {}
    c["ident"] = np.eye(128, dtype=np.float32)
    pos = np.arange(S, dtype=np.float32)
    inv0 = (1.0 / (10000.0 ** (np.arange(0, 64, 2, dtype=np.float32) / 64))).astype(np.float32)
    ang0 = pos[None, :] * inv0[:, None]
    idx = (np.arange(128) % 64) % 32
    c["cos0"] = np.cos(ang0)[idx].astype(np.float32)
    c["sin0"] = np.sin(ang0)[idx].astype(np.float32)
    R0 = np.zeros((128, 128), np.float32)
    for m in range(128):
        if (m % 64) < 32:
            R0[m + 32, m] = -1.0
        else:
            R0[m - 32, m] = 1.0
    c["R0"] = R0
    bo = np.zeros((128, 128), np.float32)
    bo[:64, :64] = 1.0
    bo[64:, 64:] = 1.0
    c["bones"] = bo
    inv1 = (1.0 / (10000.0 ** (np.arange(0, 32, 2, dtype=np.float32) / 32))).astype(np.float32)
    ang1 = pos[None, :] * inv1[:, None]
    cos1 = np.zeros((128, S), np.float32)
    cos1[:64] = 1.0
    sin1 = np.zeros((128, S), np.float32)
    cos1[64:96] = np.cos(ang1)[np.arange(32) % 16]
    sin1[64:96] = np.sin(ang1)[np.arange(32) % 16]
    c["cos1"] = cos1
    c["sin1"] = sin1
    R1 = np.zeros((96, 96), np.float32)
    for m in range(64, 96):
        if m - 64 < 16:
            R1[m + 16, m] = -1.0
        else:
            R1[m - 16, m] = 1.0
    R1p = np.zeros((128, 128), np.float32)
    R1p[:96, :96] = R1
    c["R1"] = R1p
    o96 = np.zeros((128, 128), np.float32)
    o96[:96, :96] = 1.0
    c["ones96"] = o96
    sh = np.zeros((128, 128), np.float32)
    for i in range(32):
        sh[i, 64 + i] = 1.0
    c["shpe"] = sh
    rr_h = np.arange(128) // 32
    nn_h = np.arange(128) // 64
    for m in range(2):
        ind = (rr_h[:, None] == (2 * m + nn_h)[None, :]).astype(np.float32)
        c[f"indRN{m}"] = ind
        c[f"indNR{m}"] = np.ascontiguousarray(ind.T)
    c["bones32"] = (rr_h[:, None] == rr_h[None, :]).astype(np.float32)
    R32 = np.zeros((128, 128), np.float32)
    for m in range(128):
        if (m % 32) < 16:
            R32[m + 16, m] = -1.0
        else:
            R32[m - 16, m] = 1.0
    c["R32"] = R32
    rep32 = np.zeros((128, 128), np.float32)
    for m in range(128):
        rep32[m % 32, m] = 1.0
    c["rep32"] = rep32
    c["cosR"] = np.cos(ang1)[(np.arange(128) % 32) % 16].astype(np.float32)
    c["sinR"] = np.sin(ang1)[(np.arange(128) % 32) % 16].astype(np.float32)
    kk = np.arange(128)[:, None, None]
    rr = np.arange(4)[None, :, None]
    jj = np.arange(512)[None, None, :]
    c["cmask"] = np.where(rr * 128 + kk <= jj, 0.0, NEG).astype(np.float32)
    Z = np.zeros((64, 64 * 128), np.float32)
    for p in range(64):
        Z[p, p * 128:(p + 1) * 128] = 1.0
    c["Zsel"] = Z
    tt = np.arange(16)[:, None]
    nn = (np.arange(64) % 8)[None, :]
    own = tt // 2
    c["negm"] = np.where(nn < own, 0.0, -1e30).astype(np.float32).reshape(1, 16 * 64)
    c["elig"] = (nn < own).astype(np.float32).reshape(1, 16 * 64)
    c["ownm"] = (nn == own).astype(np.float32).reshape(1, 16 * 64)
    d_ = np.arange(128)[:, None]
    l_ = np.arange(128)[None, :]
    c["triLE"] = (d_ <= l_).astype(np.float32)
    c["strict"] = (d_ > l_).astype(np.float32)
    c["tribias"] = np.tile(np.where(d_ > l_, NEG, 0.0).astype(np.float32), (1, 4))
    c["onesF"] = np.ones((128, 128), np.float32)
    return c


def host_params(inp):
    p = {}
    f = lambda a: np.ascontiguousarray(np.asarray(a, dtype=np.float32))
    for l in range(2):
        p[f"mixg{l}"] = f(inp["mix_norm"][l].reshape(1, D))
        p[f"ffng{l}"] = f(inp["ffn_norm"][l].reshape(1, D))
        p[f"fcw{l}"] = f(inp["ffn_conv_w"][l].T.reshape(44, 128, 3).transpose(1, 0, 2))
        p[f"fcb{l}"] = f(inp["ffn_conv_b"][l].reshape(44, 128).T)
        p[f"wup{l}"] = f(inp["ffn_w_up"][l])
        p[f"wdn{l}"] = f(inp["ffn_w_down"][l])
    p["w_in0"] = f(inp["ev_w_in"][0])
    p["w_out0"] = f(inp["ev_w_out"][0])
    p["gq0"] = f(np.tile(inp["ev_q_norm"][0], 2).reshape(128, 1))
    p["gk0"] = f(np.tile(inp["ev_k_norm"][0], 2).reshape(128, 1))
    p["cw0"] = f(inp["ev_conv_w"][0].T.reshape(8, 128, 4).transpose(1, 0, 2))
    p["cb0"] = f(inp["ev_conv_b"][0].reshape(8, 128).T)
    p["dtb"] = f(inp["ev_dt_bias"][0].reshape(1, 8))
    p["alog"] = f(inp["ev_a_log"][0].reshape(1, 8))
    p["dskip"] = f(inp["ev_d_skip"][0].reshape(1, 8))
    p["ssdg"] = f(inp["ev_ssd_norm"][0].reshape(1, 512))
    p["w_in1"] = f(inp["od_w_in"][0])
    p["w_uq"] = f(inp["od_w_uq"][0])
    p["w_ukv"] = f(inp["od_w_ukv"][0])
    p["w_out1"] = f(inp["od_w_out"][0])
    p["gqa"] = f(inp["od_q_a_norm"][0].reshape(4, 128).T)
    p["gkva"] = f(inp["od_kv_a_norm"][0].reshape(2, 128).T)
    for nm, src in (("q", inp["od_q_norm"][0]), ("k", inp["od_k_norm"][0])):
        p[f"g{nm}N"] = f(np.tile(src[0:64], 2).reshape(128, 1))
        p[f"g{nm}R"] = f(np.tile(src[64:96], 4).reshape(128, 1))
    return p


def norm_stats(k, C, xs, t, gb, hn):
    k.act(hn[:, :], xs[:, t, :], AF.Square, [(xs, t)], [hn, (C["ss"], t)], accum_out=C["ss"][:, t:t + 1])
    k.act(C["rs"][:, t:t + 1], C["ss"][:, t:t + 1], AF.Ln, [(C["ss"], t), C["eps"]], [(C["rs"], t)], scale=1.0 / D, bias=C["eps"][:, :])
    k.act(C["rs"][:, t:t + 1], C["rs"][:, t:t + 1], AF.Exp, [(C["rs"], t)], [(C["rs"], t)], scale=-0.5)
    k.stt(hn[:, :], xs[:, t, :], C["rs"][:, t:t + 1], gb[:, :], ALU.mult, ALU.mult, [(xs, t), (C["rs"], t), gb], [hn])


def norm_tr(k, C, hn, hnT, t):
    ptr = k.banks[7]
    pv = ptr.ap.bitcast(BF16)
    for c in range(8):
        k.tr(pv[:, c * 128:(c + 1) * 128], hn[:, c * 128:(c + 1) * 128], C["idb"][:, :], [hn, C["idb"]], [ptr])
    k.copy("act", hnT[:, :, t * 128:(t + 1) * 128], pv.rearrange("p (c n) -> p c n", c=8), [ptr], [(hnT, t)])


def norm_T(k, C, xs, ntile, gb, hnT):
    ptr = k.banks[7]
    pv = ptr.ap.bitcast(BF16)
    for t in range(ntile):
        hn = C["hn"][t % 2]
        k.act(C["junk"][:, :], xs[:, t, :], AF.Square, [(xs, t)], [C["junk"], (C["ss"], t)], accum_out=C["ss"][:, t:t + 1])
        k.act(C["rs"][:, t:t + 1], C["ss"][:, t:t + 1], AF.Ln, [(C["ss"], t), C["eps"]], [(C["rs"], t)], scale=1.0 / D, bias=C["eps"][:, :])
        k.act(C["rs"][:, t:t + 1], C["rs"][:, t:t + 1], AF.Exp, [(C["rs"], t)], [(C["rs"], t)], scale=-0.5)
        k.stt(hn[:, :], xs[:, t, :], C["rs"][:, t:t + 1], gb[:, :], ALU.mult, ALU.mult, [(xs, t), (C["rs"], t), gb], [hn])
        for c in range(8):
            k.tr(pv[:, c * 128:(c + 1) * 128], hn[:, c * 128:(c + 1) * 128], C["idb"][:, :], [hn, C["idb"]], [ptr])
        k.copy("act", hnT[:, :, t * 128:(t + 1) * 128], pv.rearrange("p (c n) -> p c n", c=8), [ptr], [(hnT, t)])


def common_consts(k, G):
    C = {}
    C["idb"] = k.sb("idb", [128, 128], BF16)
    k.dma("pool", C["idb"][:, :], G["ident"][:, :], writes=[C["idb"]], lane="c0")
    C["eps"] = k.sb("eps", [128, 1])
    k.memset("dve", C["eps"][:, :], EPS, [C["eps"]])
    C["one"] = k.sb("one", [128, 1])
    k.memset("dve", C["one"][:, :], 1.0, [C["one"]])
    C["ss"] = k.sb("ss", [128, 4])
    C["rs"] = k.sb("rs", [128, 4])
    C["hn"] = [k.sb(f"hn{i}", [128, 1024], BF16) for i in range(2)]
    k.persist_mark()
    return C


def load_w(k, dst, src, nchunk, lane, col0=None, col1=None):
    for c in range(nchunk):
        s = src[c * 128:(c + 1) * 128, :] if col0 is None else src[c * 128:(c + 1) * 128, col0:col1]
        k.dma("pool", dst[:, c, :], s, writes=[dst], lane=lane, group=(c > 0))


def load_w_blocks(k, dst, src, nchunk, blocks, lane):
    for bi, (c0, c1) in enumerate(blocks):
        k.dma("pool", dst[:, :, c0:c1], src[:, c0:c1].rearrange("(c p) n -> p c n", p=128), writes=[(dst, bi)], lane=f"{lane}_{bi}")


def phase_A0(k, C, G):
    k.phase("A0")
    Win = k.sb("Win", [128, 8, 3080], BF16)
    wblocks = [(0, 512), (512, 1024), (2048, 2560), (2560, 3072), (1024, 1536), (1536, 2048), (3072, 3080)]
    load_w_blocks(k, Win, G["w_in0"], 8, wblocks, "wi")
    wblk = lambda col: [i for i, (a, b_) in enumerate(wblocks) if a <= col < b_][0]
    gb = k.sb("gb", [128, D])
    k.dma("sp", gb[:, :], G["mixg0"][:, :].broadcast_to([128, D]), writes=[gb], lane="c1")
    cosT = k.sb("cosT", [128, S]); sinT = k.sb("sinT", [128, S])
    k.dma("sp", cosT[:, :], G["cos0"][:, :], writes=[cosT], lane="c2")
    k.dma("sp", sinT[:, :], G["sin0"][:, :], writes=[sinT], lane="c3")
    R0 = k.sb("R0", [128, 128], BF16); bones = k.sb("bones", [128, 128], BF16)
    k.dma("pool", R0[:, :], G["R0"][:, :], writes=[R0], lane="c4")
    k.dma("pool", bones[:, :], G["bones"][:, :], writes=[bones], lane="c5")
    gq = k.sb("gq", [128, 1]); gk = k.sb("gk", [128, 1]); cw = k.sb("cw", [128, 8, 4]); cb = k.sb("cb", [128, 8])
    k.dma("sp", gq[:, :], G["gq0"][:, :], writes=[gq], lane="c1")
    k.dma("sp", gk[:, :], G["gk0"][:, :], writes=[gk], lane="c2")
    k.dma("sp", cw[:, :, :], G["cw0"][:, :, :], writes=[cw], lane="c3")
    k.dma("sp", cb[:, :], G["cb0"][:, :], writes=[cb], lane="c1")
    rawp = k.sb("rawp", [128, 8, 515])
    xs2 = [k.sb(f"xs{i}", [128, 4, D]) for i in range(2)]
    hnT2 = [k.sb(f"hnT{i}", [128, 8, 512], BF16) for i in range(2)]
    hn4 = C["hn"] + [k.sb(f"hnx{i}", [128, 1024], BF16) for i in range(2)]
    NW = 4
    W2 = lambda n, dt=F32, nb=4: [k.sb(f"{n}{i}", [128, 512], dt) for i in range(nb)]
    sq = W2("sq", BF16); rstd = W2("rstd"); qn = W2("qn", BF16); t1 = W2("t1", BF16); t2 = W2("t2", BF16); qo = W2("qo", BF16)
    acc = W2("acc"); xo = W2("xo", BF16); zo = W2("zo", BF16, 2)
    vo = [k.sb(f"vo{i}", [128, 8, 65], BF16) for i in range(2)]
    for b_ in range(2):
        k.memset("pool", vo[b_][:, :, 64:65], 1.0, [vo[b_]])
    dto = [k.sb(f"dto{i}", [128, 8]) for i in range(2)]
    pb = k.banks
    it = 0
    NCH = 2 * NSEQ * 2

    def xload(i):
        s, pos = i // 4, (i % 4) * 512
        k.dma("sp", xs2[i % 2][:, :, :], G["x"][s, pos:pos + 512, :].rearrange("(t p) d -> p t d", p=128), writes=[xs2[i % 2]], lane=f"xa{i % 2}")

    xload(0)
    for t in range(4):
        norm_stats(k, C, xs2[0], t, gb, hn4[t])
        norm_tr(k, C, hn4[t], hnT2[0], t)
    for i in range(NCH):
        s, pos = i // 4, (i % 4) * 512
        hnT = hnT2[i % 2]
        base = it
        it += 16

        def st0(cc, s=s, pos=pos, base=base):
            col0 = cc * 128 if cc < 8 else 2048 + (cc - 8) * 128
            acc_b = pb[(base + cc) % 3]
            b = (base + cc) % NW
            for c in range(8):
                k.mm(acc_b[:, :], Win[:, c, col0:col0 + 128], hnT[:, c, :], c == 0, c == 7, [(Win, wblk(col0)), hnT], [acc_b])
            if cc < 8:
                k.act(sq[b][:, :], acc_b[:, :], AF.Square, [acc_b], [sq[b]])
            else:
                j = cc - 8
                rp = rawp
                if pos == 0:
                    k.memset("pool", rp[:, j, 0:3], 0.0, [(rp, j)])
                else:
                    k.copy("pool", rp[:, j, 0:3], rp[:, j, 512:515], [(rp, j)], [(rp, j)])
                k.copy("act", rp[:, j, 3:515], acc_b[:, :], [acc_b], [(rp, j)])
                k.act(acc[b][:, :], acc_b[:, :], AF.Identity, [acc_b, cw, cb], [acc[b]], scale=cw[:, j, 3:4], bias=cb[:, j:j + 1])

        def st1(cc, s=s, pos=pos, base=base):
            acc_b = pb[(base + cc) % 3]
            b = (base + cc) % NW
            if cc < 8:
                g = gq if cc < 4 else gk
                pss = pb[3 + (base + cc) % 2]
                k.mm(pss[:, :], bones[:, :], sq[b][:, :], True, True, [bones, sq[b]], [pss])
                k.act(rstd[b][:, :], pss[:, :], AF.Ln, [pss, C["eps"]], [rstd[b]], scale=1.0 / 64, bias=C["eps"][:, :])
                k.act(rstd[b][:, :], rstd[b][:, :], AF.Exp, [rstd[b]], [rstd[b]], scale=-0.5)
                k.stt(qn[b][:, :], acc_b[:, :], g[:, 0:1], rstd[b][:, :], ALU.mult, ALU.mult, [acc_b, g, rstd[b]], [qn[b]])
            else:
                j = cc - 8
                rp = rawp
                for q_ in range(0, 3):
                    k.stt(acc[b][:, :], rp[:, j, q_:q_ + 512], cw[:, j, q_:q_ + 1], acc[b][:, :], ALU.mult, ALU.add, [(rp, j), cw, acc[b]], [acc[b]])
                k.act(xo[b][:, :], acc[b][:, :], AF.Silu, [acc[b]], [xo[b]])
                k.dma("sp", G["xbcT"][s, j * 128:(j + 1) * 128, pos:pos + 512], xo[b][:, :], reads=[xo[b]], writes=[(G["xbcT"], (s, j, pos))], lane=f"xo{b}")

        def st2(cc, s=s, pos=pos, base=base):
            if cc >= 8:
                return
            b = (base + cc) % NW
            dst = (G["qT0"] if cc < 4 else G["kT0"])
            j = cc % 4
            prx = pb[5 + (base + cc) % 2]
            k.mm(prx[:, :], R0[:, :], qn[b][:, :], True, True, [R0, qn[b]], [prx])
            k.tt("pool", t1[b][:, :], qn[b][:, :], cosT[:, pos:pos + 512], ALU.mult, [qn[b], cosT], [t1[b]])
            k.tt("dve", t2[b][:, :], prx[:, :], sinT[:, pos:pos + 512], ALU.mult, [prx, sinT], [t2[b]])
            k.tt("dve", qo[b][:, :], t1[b][:, :], t2[b][:, :], ALU.add, [t1[b], t2[b]], [qo[b]])
            k.dma("sp", dst[s, j * 128:(j + 1) * 128, pos:pos + 512], qo[b][:, :], reads=[qo[b]], writes=[(dst, (s, j, pos))], lane=f"qo{b}")

        pipeline(16, [(0, st0), (1, st1), (2, st2)])
        if i + 1 < NCH:
            xload(i + 1)
            for t in range(4):
                norm_stats(k, C, xs2[(i + 1) % 2], t, gb, hn4[t])
        for t in range(4):
            b = t % 2
            r0 = pos + t * 128
            pv_, pz_, pd_ = pb[3 + t % 2], pb[5 + t % 2], pb[7]
            for c in range(8):
                k.mm(pv_[:, :], hnT[:, c, t * 128:(t + 1) * 128], Win[:, c, 1024:1536], c == 0, c == 7, [hnT, (Win, 4)], [pv_])
            for c in range(8):
                k.mm(pz_[:, :], hnT[:, c, t * 128:(t + 1) * 128], Win[:, c, 1536:2048], c == 0, c == 7, [hnT, (Win, 5)], [pz_])
            for c in range(8):
                k.mm(pd_[:, 0:8], hnT[:, c, t * 128:(t + 1) * 128], Win[:, c, 3072:3080], c == 0, c == 7, [hnT, (Win, 6)], [pd_])
            k.copy("dve", vo[b][:, :, 0:64], pv_[:, :].rearrange("p (h d) -> p h d", h=8), [pv_], [vo[b]])
            k.act(zo[b][:, :], pz_[:, :], AF.Silu, [pz_], [zo[b]])
            k.copy("dve", dto[b][:, :], pd_[:, 0:8], [pd_], [dto[b]])
            k.dma("sp", G["v0"][s, r0:r0 + 128, :], vo[b][:, :, :].rearrange("p h d -> p (h d)"), reads=[vo[b]], writes=[(G["v0"], (s, r0))], lane=f"vo{b}")
            k.dma("sp", G["z0"][s, r0:r0 + 128, :], zo[b][:, :], reads=[zo[b]], writes=[(G["z0"], (s, r0))], lane=f"zo{b}")
            k.dma("sp", G["dt0"][s, :, r0 // 128, :], dto[b][:, :], reads=[dto[b]], writes=[(G["dt0"], (s, r0))], lane=f"do{b}")
        if i + 1 < NCH:
            for t in range(4):
                norm_tr(k, C, hn4[t], hnT2[(i + 1) % 2], t)


def attention_run(k, C, heads, scale, cmask, Pt):
    pb = k.banks
    steps = [(hd, c, kt) for hd in heads for c in range(4) for kt in range(4 * c + 4)]
    st = {"si": 0, "oi": 0}

    def emit_S(i):
        hd, c, kt = steps[i]
        r = kt - 4 * c
        j0 = max(r, 0) * 128
        Sb = pb[i % 3]
        bias = hd.get("bias")
        k.mm(Sb[:, j0:512], hd["k"](kt), hd["q"](c, j0), True, bias is None and r < 0, [hd["kT"], hd["qT"]], [Sb])
        if bias is not None:
            Z, biasT, hsel = bias
            zi = (hsel * 8 + kt // 2) * 128
            k.mm(Sb[:, j0:512], Z[:, zi:zi + 128], biasT[:, c * 512 + j0:(c + 1) * 512], False, r < 0, [Z, biasT], [Sb])
        if r >= 0:
            k.mm(Sb[:, j0:512], C["idb"][:, :], cmask[:, r, j0:512], False, True, [C["idb"], cmask], [Sb])

    emit_S(0)
    for i, (hd, c, kt) in enumerate(steps):
        if i + 1 < len(steps):
            emit_S(i + 1)
        r = kt - 4 * c
        j0 = max(r, 0) * 128
        Sb = pb[i % 3]
        P = Pt[i % 3]
        if kt == 0:
            st["oi"] += 1
        O = pb[3 + st["oi"] % 2]
        Ov = O.ap[:, 0:260].rearrange("p (a b) -> p a b", a=4)
        k.act(P[:, j0:512], Sb[:, j0:512], AF.Exp, [Sb], [P], scale=scale)
        for qi in range(max(r, 0), 4):
            k.mm(Ov[:, qi, :], P[:, qi * 128:(qi + 1) * 128], hd["v"](kt), kt == 0 and qi == 0, kt == 4 * c + qi, [P, hd["Va"]], [O], sgc=True)
        if kt == 4 * c + 3:
            rec = C["rec"][st["oi"] % 2]
            omix, ocol = hd["omix"], hd["ocol"]
            k.op("dve", lambda e, rec=rec, Ov=Ov: e.reciprocal(out=rec[:, :], in_=Ov[:, :, 64]), [O], [rec])
            k.tt("dve", omix[:, 4 * c:4 * c + 4, ocol:ocol + 64], Ov[:, :, 0:64], bview(rec.ap, [[1, 4], [0, 64]]), ALU.mult,
                 [O, rec], [(omix, (c, ocol))])
            if hd.get("after") is not None and c == 3:
                hd["after"]()


def phase_B0(k, C, G):
    k.phase("B0")
    pb = k.banks
    kTs = [k.sb(f"kTz{i}", [128, 8, S], BF16) for i in range(2)]
    qTs = [k.sb(f"qT{i}", [128, 4, S], BF16) for i in range(2)]
    biasTs = [k.sb(f"biasT{i}", [64, S], BF16) for i in range(2)]
    Vaug = k.sb("Vaug", [128, 16, 8, 65], BF16)
    Z = k.sb("Z", [64, 64 * 128], BF16)
    k.dma("pool", Z[:, :], G["Zsel"][:, :], writes=[Z], lane="c1")
    cmask = k.sb("cmask", [128, 4, 512], BF16)
    k.dma("pool", cmask[:, :, :], G["cmask"][:, :, :], writes=[cmask], lane="c2")
    negm = k.sb("negm", [128, 16, 64]); elig = k.sb("elig", [128, 16, 64]); ownm = k.sb("ownm", [128, 16, 64])
    for t_, nm, ln in ((negm, "negm", "c3"), (elig, "elig", "c4"), (ownm, "ownm", "c5")):
        k.dma("sp", t_[:, :, :].rearrange("p a b -> p (a b)"), G[nm][:, :].broadcast_to([128, 1024]), writes=[t_], lane=ln)
    km = k.sb("km", [128, 8, 8]); kmb = k.sb("kmb", [128, 8, 8], BF16)
    Gs = [k.sb(f"G{i}", [128, 128, 8]) for i in range(3)]
    m_ = k.sb("m", [128, 128]); tmask = k.sb("tmask", [128, 128, 8]); bq = k.sb("bq", [128, 16, 128], BF16)
    Pt = [k.sb(f"Pt{i}", [128, 512], BF16) for i in range(3)]
    omix = k.sb("omix", [128, 16, 512], BF16)
    C["rec"] = [k.sb(f"rec{i}", [128, 4]) for i in range(2)]
    k.memset("pool", bq[:, :, :], 0.0, [bq])
    for b_ in range(2):
        k.memset("pool", kTs[b_][:, :, :], 0.0, [kTs[b_]])

    def load_kq(s):
        kT, qT = kTs[s % 2], qTs[s % 2]
        for cc in range(4):
            k.dma("sp", kT[0:64, 2 * cc, :], G["kT0"][s, cc * 128:cc * 128 + 64, :], reads=[G["kT0"]], writes=[kT], lane=f"kT{s % 2}", group=(cc > 0))
            k.dma("sp", kT[64:128, 2 * cc + 1, :], G["kT0"][s, cc * 128 + 64:(cc + 1) * 128, :], reads=[G["kT0"]], writes=[kT], lane=f"kT{s % 2}", group=True)
        k.dma("sp", qT[:, :, :], G["qT0"][s].rearrange("(c p) t -> p c t", p=128), reads=[G["qT0"]], writes=[qT], lane=f"qT{s % 2}")

    def load_v(s):
        k.dma("sp", Vaug[:, :, :, :].rearrange("p t h d -> p t (h d)"), G["v0"][s].rearrange("(t p) d -> p t d", p=128), reads=[G["v0"]], writes=[Vaug], lane="V")

    def select(s, part=None):
        if part is None or part == 0:
            select_a(s)
        if part is None or part == 1:
            select_b(s)
        if part is None or part == 2:
            select_c(s)

    def select_a(s):
        kT = kTs[s % 2]
        k.op("dve", lambda e: e.tensor_reduce(out=km[:, :, :], in_=kT[:, :, :].rearrange("p c (n l) -> p c n l", n=8), axis=AX.X, op=ALU.add),
             [kT], [km])
        k.ts("dve", kmb[:, :, :], km[:, :, :], 1.0 / 256, None, ALU.mult, None, [km], [kmb])

    def select_b(s):
        qT = qTs[s % 2]
        pgs = (pb[5], pb[6])
        for tt_ in range(16):
            pg = pgs[tt_ // 8]
            o_ = (tt_ % 8) * 64
            for h in range(8):
                k.mm(pg[:, o_ + h * 8:o_ + (h + 1) * 8], qT[:, h // 2, tt_ * 128:(tt_ + 1) * 128], kmb[:, h, :], True, True, [qT, kmb], [pg])
        G0, G1, G2 = Gs
        fl = lambda t_: t_[:, :, :].rearrange("p a b -> p (a b)")
        mb = bview(m_.ap, [[1, 128], [0, 8]])
        for hf in range(2):
            k.tt("dve", fl(G0)[:, hf * 512:(hf + 1) * 512], pgs[hf][:, :], fl(negm)[:, hf * 512:(hf + 1) * 512], ALU.add, [pgs[hf], negm], [G0])
        cur = G0
        for nxt in (G1, G2):
            k.op("dve", lambda e, cur=cur: e.tensor_reduce(out=m_[:, :], in_=cur[:, :, :], axis=AX.X, op=ALU.max), [cur], [m_])
            k.tt("dve", tmask[:, :, :], cur[:, :, :], mb, ALU.is_ge, [cur, m_], [tmask])
            k.stt(fl(nxt), fl(tmask), -1e30, fl(cur), ALU.mult, ALU.add, [tmask, cur], [nxt])
            cur = nxt
        k.op("dve", lambda e, cur=cur: e.tensor_reduce(out=m_[:, :], in_=cur[:, :, :], axis=AX.X, op=ALU.max), [cur], [m_])
        k.tt("dve", tmask[:, :, :], G0[:, :, :], mb, ALU.is_ge, [G0, m_], [tmask])
        k.tt("dve", fl(tmask), fl(tmask), fl(elig), ALU.mult, [tmask, elig], [tmask])
        k.tt("dve", fl(tmask), fl(tmask), fl(ownm), ALU.add, [tmask, ownm], [tmask])
        k.ts("dve", bq[:, :, 0:64], tmask[:, :, :].rearrange("p (t a) b -> p t (a b)", t=16), -1.0, -NEG, ALU.add, ALU.mult, [tmask], [bq])

    def select_c(s):
        biasT = biasTs[s % 2]
        for tt_ in range(16):
            pt_ = pb[7]
            ptv = pt_.ap.bitcast(BF16)
            k.tr(ptv[:, 0:128], bq[:, tt_, :], C["idb"][:, :], [bq, C["idb"]], [pt_])
            k.copy("act", biasT[:, tt_ * 128:(tt_ + 1) * 128], ptv[0:64, 0:128], [pt_], [biasT])

    load_kq(0)
    load_v(0)
    select(0)
    for s in range(NSEQ):
        kT, qT, biasT = kTs[s % 2], qTs[s % 2], biasTs[s % 2]
        heads = []
        for h in range(8):
            cc = h // 2
            heads.append({"kT": kT, "qT": qT, "Va": Vaug, "omix": omix, "ocol": h * 64, "bias": (Z, biasT, h),
                          "k": (lambda kt, h=h, kT=kT: kT[:, h, kt * 128:(kt + 1) * 128]),
                          "q": (lambda c, j0, cc=cc, qT=qT: qT[:, cc, c * 512 + j0:(c + 1) * 512]),
                          "v": (lambda kt, h=h: Vaug[:, kt, h, :])})
        if s + 1 < NSEQ:
            heads[0]["after"] = (lambda s=s: load_kq(s + 1))
            heads[2]["after"] = (lambda s=s: select(s + 1, 0))
            heads[4]["after"] = (lambda s=s: select(s + 1, 1))
            heads[6]["after"] = (lambda s=s: select(s + 1, 2))
        attention_run(k, C, heads, 0.125, cmask, Pt)
        k.dma("sp", G["mix0"][s, :, 0:512].rearrange("(t p) d -> p t d", p=128), omix[:, :, :], reads=[omix], writes=[(G["mix0"], (s, "a"))], lane="om")
        if s + 1 < NSEQ:
            load_v(s + 1)


def phase_C0(k, C, G):
    k.phase("C0")
    pb = k.banks
    dtb = k.sb("dtb", [128, 8]); Ab = k.sb("Ab", [128, 8]); dsk = k.sb("dsk", [128, 8]); gn = k.sb("gn", [128, 512])
    for t_, nm, ln, w in ((dtb, "dtb", "c1", 8), (Ab, "alog", "c2", 8), (dsk, "dskip", "c3", 8), (gn, "ssdg", "c4", 512)):
        k.dma("sp", t_[:, :], G[nm][:, :].broadcast_to([128, w]), writes=[t_], lane=ln)
    triLE = k.sb("triLE", [128, 128]); strict = k.sb("strict", [128, 128]); tribias = k.sb("tribias", [128, 512], BF16); onesF = k.sb("onesF", [128, 128])
    k.dma("sp", triLE[:, :], G["triLE"][:, :], writes=[triLE], lane="c5")
    k.dma("sp", strict[:, :], G["strict"][:, :], writes=[strict], lane="c1")
    k.dma("pool", tribias[:, :], G["tribias"][:, :], writes=[tribias], lane="c2")
    k.dma("sp", onesF[:, :], G["onesF"][:, :], writes=[onesF], lane="c3")
    k.act(Ab[:, :], Ab[:, :], AF.Exp, [Ab], [Ab])
    k.ts("dve", Ab[:, :], Ab[:, :], -1.0, None, ALU.mult, None, [Ab], [Ab])
    C["junk"] = k.sb("junk", [128, 1024], BF16)
    ob = outproj_alloc(k, G, "w_out0")
    v3 = lambda t_: t_[:, :].rearrange("p (h d) -> p h d", h=8)

    def mk_src(am, ym):
        return lambda tt, c: (am[:, tt, c * 128:(c + 1) * 128], am) if c < 4 else (ym[:, tt, (c - 4) * 128:(c - 3) * 128], ym)

    ctx = []
    for s in range(NSEQ):
        T_ = {}
        T_["xb"] = k.sb(f"xb{s}", [128, 8, 1024], BF16); T_["zs"] = k.sb(f"zs{s}", [128, 8, 512], BF16)
        T_["dt"] = k.sb(f"dt{s}", [128, 16, 8]); T_["a_"] = k.sb(f"a{s}", [128, 16, 8])
        T_["xtok"] = k.sb(f"xtok{s}", [128, 512]); T_["Btok"] = k.sb(f"Btok{s}", [128, 256], BF16)
        T_["xd"] = k.sb(f"xd{s}", [128, 8, 64], BF16); T_["xdd"] = k.sb(f"xdd{s}", [128, 8, 64], BF16)
        T_["cs"] = k.sb(f"cs{s}", [128, 16]); T_["eacs"] = k.sb(f"eacs{s}", [128, 8]); T_["dte"] = k.sb(f"dte{s}", [128, 8]); T_["etot"] = k.sb(f"etot{s}", [128, 8])
        T_["amask"] = k.sb(f"amask{s}", [128, 8, 128])
        T_["Lt"] = k.sb(f"Lt{s}", [128, 8, 128]); T_["Mt"] = k.sb(f"Mt{s}", [128, 8, 128], BF16)
        T_["H"] = k.sb(f"H{s}", [128, 8, 64]); T_["Hb"] = k.sb(f"Hb{s}", [128, 8, 64], BF16)
        T_["y"] = k.sb(f"y{s}", [128, 512]); T_["tmp"] = k.sb(f"tmp{s}", [128, 512]); T_["tmp2"] = k.sb(f"tmp2{s}", [128, 512])
        T_["ssq"] = k.sb(f"ssq{s}", [128, 2]); T_["rsq"] = k.sb(f"rsq{s}", [128, 2])
        T_["ymix"] = k.sb(f"ymix{s}", [128, 16, 512], BF16); T_["amix"] = k.sb(f"amix{s}", [128, 16, 512], BF16)
        ctx.append(T_)

    def load_half(s, hf):
        T_ = ctx[s]
        k.dma("sp", T_["xb"][:, :, :], G["xbcT"][s, :, hf * 1024:(hf + 1) * 1024].rearrange("(c p) t -> p c t", p=128), reads=[G["xbcT"]], writes=[T_["xb"]], lane=f"xb{s}")
        k.dma("sp", T_["zs"][:, :, :], G["z0"][s, hf * 1024:(hf + 1) * 1024, :].rearrange("(t p) d -> p t d", p=128), reads=[G["z0"]], writes=[T_["zs"]], lane=f"zs{s}")

    for s in range(NSEQ):
        T_ = ctx[s]
        dt, a_, H, Hb, amix = T_["dt"], T_["a_"], T_["H"], T_["Hb"], T_["amix"]
        load_half(s, 0)
        k.dma("sp", amix[:, :, :], G["mix0"][s, :, 0:512].rearrange("(t p) d -> p t d", p=128), reads=[G["mix0"]], writes=[amix], lane=f"am{s}")
        k.dma("sp", dt[:, :, :], G["dt0"][s], reads=[G["dt0"]], writes=[dt], lane=f"dtl{s}")
        k.tt("dve", dt[:, :, :], dt[:, :, :], bview(dtb.ap, [[0, 16], [1, 8]]), ALU.add, [dt, dtb], [dt])
        k.act(a_[:, :, :], dt[:, :, :], AF.Abs, [dt], [a_])
        k.act(a_[:, :, :], a_[:, :, :], AF.Exp, [a_], [a_], scale=-1.0)
        k.act(a_[:, :, :], a_[:, :, :], AF.Ln, [a_, C["one"]], [a_], bias=C["one"][:, :])
        k.ts("dve", dt[:, :, :], dt[:, :, :], 0.0, None, ALU.max, None, [dt], [dt])
        k.tt("dve", dt[:, :, :], dt[:, :, :], a_[:, :, :], ALU.add, [dt, a_], [dt])
        k.tt("dve", a_[:, :, :], dt[:, :, :], bview(Ab.ap, [[0, 16], [1, 8]]), ALU.mult, [dt, Ab], [a_])
        k.memset("dve", H[:, :, :], 0.0, [H])
        k.memset("pool", Hb[:, :, :], 0.0, [Hb])

    def body(s, ch):
        T_ = ctx[s]
        xb, zs, dt, a_, xtok, Btok, xd, xdd = T_["xb"], T_["zs"], T_["dt"], T_["a_"], T_["xtok"], T_["Btok"], T_["xd"], T_["xdd"]
        cs, eacs, dte, etot, amask, Lt, Mt = T_["cs"], T_["eacs"], T_["dte"], T_["etot"], T_["amask"], T_["Lt"], T_["Mt"]
        H, Hb, y, tmp, tmp2, ssq, rsq, ymix = T_["H"], T_["Hb"], T_["y"], T_["tmp"], T_["tmp2"], T_["ssq"], T_["rsq"], T_["ymix"]
        cols = slice((ch % 8) * 128, (ch % 8 + 1) * 128)
        X = (pb[0], pb[1], pb[2]) if s == 0 else (pb[3], pb[4], pb[5])
        ptr = X[0]
        pv = ptr.ap.bitcast(BF16)
        for j in range(6):
            k.tr(pv[:, j * 128:(j + 1) * 128], xb[:, j, cols], C["idb"][:, :], [xb, C["idb"]], [ptr])
        k.copy("act", xtok[:, :], pv[:, 0:512], [ptr], [xtok])
        k.copy("dve", Btok[:, :], pv[:, 512:768], [ptr], [Btok])
        dtc = bview(dt[:, ch, :], [[1, 8], [0, 64]])
        k.tt("dve", xd[:, :, :], v3(xtok), dtc, ALU.mult, [xtok, dt], [xd])
        yield
        pc = X[1]
        k.mm(pc[:, 0:8], triLE[:, :], a_[:, ch, :], True, True, [triLE, a_], [pc])
        k.mm(pc[:, 8:16], onesF[:, :], a_[:, ch, :], True, True, [onesF, a_], [pc])
        k.copy("dve", cs[:, :], pc[:, 0:16], [pc], [cs])
        k.act(eacs[:, :], cs[:, 0:8], AF.Exp, [cs], [eacs])
        k.act(etot[:, :], cs[:, 8:16], AF.Exp, [cs], [etot])
        k.tt("dve", dte[:, :], cs[:, 8:16], cs[:, 0:8], ALU.subtract, [cs], [dte])
        k.act(dte[:, :], dte[:, :], AF.Exp, [dte], [dte])
        k.tt("pool", xdd[:, :, :], xd[:, :, :], bview(dte.ap, [[1, 8], [0, 64]]), ALU.mult, [xd, dte], [xdd])
        k.tt("dve", amask[:, :, :], bview(strict.ap, [[0, 8], [1, 128]]), bview(a_[:, ch, :], [[1, 8], [0, 128]]), ALU.mult,
             [strict, a_], [amask])
        yield
        pcb = X[2]
        for g in range(2):
            k.mm(pcb[:, g * 128:(g + 1) * 128], xb[:, 4 + g, cols], xb[:, 6 + g, cols], True, True, [xb], [pcb])
        for g in range(2):
            pL = X[g]
            k.mm(pL[:, :], C["idb"][:, :], tribias[:, :], True, False, [C["idb"], tribias], [pL], sgc=True)
            for hh in range(4):
                k.mm(pL[:, hh * 128:(hh + 1) * 128], amask[:, 4 * g + hh, :], triLE[:, :], False, True, [amask, triLE], [pL], sgc=True)
            k.act(Lt[:, 4 * g:4 * g + 4, :], pL[:, :].rearrange("p (h l) -> p h l", h=4), AF.Exp, [pL], [(Lt, g)])
            k.tt("dve", Mt[:, 4 * g:4 * g + 4, :], Lt[:, 4 * g:4 * g + 4, :], bview(pcb[:, g * 128:(g + 1) * 128], [[0, 4], [1, 128]]), ALU.mult,
                 [(Lt, g), pcb], [(Mt, g)])
        yield
        pY, pYo, pS = X[0], X[1], X[2]
        for h in range(8):
            g = h // 4
            k.mm(pY[:, h * 64:(h + 1) * 64], Mt[:, h, :], xd[:, h, :], True, True, [(Mt, g), xd], [pY])
        for h in range(8):
            g = h // 4
            k.mm(pYo[:, h * 64:(h + 1) * 64], xb[:, 6 + g, cols], Hb[:, h, :], True, True, [xb, Hb], [pYo])
        for h in range(8):
            g = h // 4
            k.mm(pS[:, h * 64:(h + 1) * 64], Btok[:, g * 128:(g + 1) * 128], xdd[:, h, :], True, True, [Btok, xdd], [pS])
        k.tt("dve", v3(tmp), pYo[:, :].rearrange("p (h d) -> p h d", h=8), bview(eacs.ap, [[1, 8], [0, 64]]), ALU.mult, [pYo, eacs], [tmp])
        k.tt("dve", y[:, :], tmp[:, :], pY[:, :], ALU.add, [tmp, pY], [y])
        k.tt("pool", v3(tmp2), v3(xtok), bview(dsk.ap, [[1, 8], [0, 64]]), ALU.mult, [xtok, dsk], [tmp2])
        k.tt("pool", y[:, :], y[:, :], tmp2[:, :], ALU.add, [y, tmp2], [y])
        k.tt("pool", y[:, :], y[:, :], zs[:, ch % 8, :], ALU.mult, [y, (zs, ch % 8)], [y])
        k.tt("dve", H[:, :, :], H[:, :, :], bview(etot.ap, [[1, 8], [0, 64]]), ALU.mult, [H, etot], [H])
        k.tt("dve", H[:, :, :], H[:, :, :], pS[:, :].rearrange("p (h d) -> p h d", h=8), ALU.add, [H, pS], [H])
        k.copy("act", Hb[:, :, :], H[:, :, :], [H], [Hb])
        yield
        for g in range(2):
            k.act(C["junk"][:, 0:256], y[:, g * 256:(g + 1) * 256], AF.Square, [y], [C["junk"], (ssq, g)], accum_out=ssq[:, g:g + 1])
        k.act(rsq[:, :], ssq[:, :], AF.Ln, [ssq, C["eps"]], [rsq], scale=1.0 / 256, bias=C["eps"][:, :])
        k.act(rsq[:, :], rsq[:, :], AF.Exp, [rsq], [rsq], scale=-0.5)
        k.tt("dve", tmp[:, :].rearrange("p (g d) -> p g d", g=2), y[:, :].rearrange("p (g d) -> p g d", g=2),
             bview(rsq.ap, [[1, 2], [0, 256]]), ALU.mult, [y, rsq], [tmp])
        k.tt("dve", ymix[:, ch, :], tmp[:, :], gn[:, :], ALU.mult, [tmp, gn], [(ymix, ch)])
        yield

    for ch in range(16):
        if ch == 8:
            for s in range(NSEQ):
                load_half(s, 1)
        gens = [body(s, ch) for s in range(NSEQ)]
        live = list(gens)
        if os.environ.get("C0_SEQ"):
            for g_ in gens:
                for _ in g_:
                    pass
            live = []
        while live:
            for g_ in list(live):
                try:
                    next(g_)
                except StopIteration:
                    live.remove(g_)
    for s in range(NSEQ):
        outproj_seq(k, C, s, mk_src(ctx[s]["amix"], ctx[s]["ymix"]), ob, G["x"], G["x1"])


def outproj_alloc(k, G, wname):
    Wo = k.sb("Wo", [128, 8, D], BF16)
    load_w(k, Wo, G[wname], 8, "wo")
    xt = [k.sb(f"xt{i}", [128, D]) for i in range(2)]
    aT = [k.sb(f"aT{i}", [128, 8, 128], BF16) for i in range(2)]
    return Wo, xt, aT


def outproj_seq(k, C, s, src, ob, xin, xout, tiles=range(16), banks=(7, 5, 6)):
    Wo, xt, aT = ob
    pb = k.banks
    for tt in tiles:
        b = tt % 2
        rows = slice(tt * 128, (tt + 1) * 128)
        k.dma("sp", xt[b][:, :], xin[s, rows, :], reads=[xin], writes=[xt[b]], lane=f"xt{b}")
        ptr = pb[banks[0]]
        pv = ptr.ap.bitcast(BF16)
        for c in range(8):
            ap, tens = src(tt, c)
            k.tr(pv[:, c * 128:(c + 1) * 128], ap, C["idb"][:, :], [tens, C["idb"]], [ptr])
        k.copy("act", aT[b][:, :, :], pv.rearrange("p (c n) -> p c n", c=8), [ptr], [aT[b]])
        for hf in range(2):
            po = pb[banks[1 + hf]]
            for c in range(8):
                k.mm(po[:, :], aT[b][:, c, :], Wo[:, c, hf * 512:(hf + 1) * 512], c == 0, c == 7, [aT[b], Wo], [po])
            k.tt("dve", xt[b][:, hf * 512:(hf + 1) * 512], xt[b][:, hf * 512:(hf + 1) * 512], po[:, :], ALU.add, [xt[b], po], [xt[b]])
        k.dma("sp", xout[s, rows, :], xt[b][:, :], reads=[xt[b]], writes=[(xout, (s, tt))], lane=f"xs{b}")


def phase_D(k, C, G, l, xin, xout):
    k.phase(f"D{l}")
    pb = k.banks
    TC = 512
    NT = TC // 128
    Wup = k.sb("Wup", [128, 8, 2 * DFF], BF16)
    jbounds = [0, 2, 8, 15, 22]
    wsrc = G[f"wup{l}"][:, :].rearrange("(c p) n -> p c n", p=128)
    for bi in range(4):
        j0, j1 = jbounds[bi], jbounds[bi + 1]
        for hf in range(2):
            c0, c1 = hf * DFF + j0 * 128, hf * DFF + j1 * 128
            k.dma("pool", Wup[:, :, c0:c1], wsrc[:, :, c0:c1], writes=[(Wup, (bi, hf))], lane=f"wu{bi}{hf}")
    ublk = lambda jj: ([bi for bi in range(4) if jbounds[bi] <= (jj % 22) < jbounds[bi + 1]][0], jj // 22)
    Wdn = k.sb("Wdn", [128, 22, D], BF16)
    for bi in range(2):
        k.dma("pool", Wdn[:, bi * 11:(bi + 1) * 11, :], G[f"wdn{l}"][bi * 1408:(bi + 1) * 1408, :].rearrange("(c p) n -> p c n", p=128),
              writes=[Wdn], lane="wd", group=(bi > 0))
    gb = k.sb("gb", [128, D])
    k.dma("sp", gb[:, :], G[f"ffng{l}"][:, :].broadcast_to([128, D]), writes=[gb], lane="c1")
    fcw = k.sb("fcw", [128, 44, 3]); fcb = k.sb("fcb", [128, 44])
    k.dma("sp", fcw[:, :, :], G[f"fcw{l}"][:, :, :], writes=[fcw], lane="c2")
    k.dma("sp", fcb[:, :], G[f"fcb{l}"][:, :], writes=[fcb], lane="c3")
    xtl = [k.sb(f"xt{i}", [128, D]) for i in range(3)]
    actT = [k.sb(f"actT{i}", [128, 8, TC], BF16) for i in range(2)]
    hid = k.sb("hid", [128, 22, TC], BF16)
    raws = [k.sb(f"raw{i}", [128, TC + 2]) for i in range(4)]
    halo = k.sb("halo", [128, 44, 2])
    accg = [k.sb(f"accg{i}", [128, TC]) for i in range(2)]
    accu = [k.sb(f"accu{i}", [128, TC]) for i in range(2)]
    nchunk = NSEQ * S // TC
    xc = [0]
    rc = [0]

    def rows_of(i, t):
        s, pos = divmod(i * TC, S)
        return s, pos + t * 128

    def norm_tile(i, t):
        s, r0 = rows_of(i, t)
        x_ = xtl[xc[0] % 3]
        xc[0] += 1
        hn = C["hn"][t % 2]
        k.dma("sp", x_[:, :], xin[s, r0:r0 + 128, :], reads=[xin], writes=[x_], lane=f"xl{xc[0] % 3}")
        k.act(hn[:, :], x_[:, :], AF.Square, [x_], [hn, (C["ss"], t)], accum_out=C["ss"][:, t:t + 1])
        k.act(C["rs"][:, t:t + 1], C["ss"][:, t:t + 1], AF.Ln, [(C["ss"], t), C["eps"]], [(C["rs"], t)], scale=1.0 / D, bias=C["eps"][:, :])
        k.act(C["rs"][:, t:t + 1], C["rs"][:, t:t + 1], AF.Exp, [(C["rs"], t)], [(C["rs"], t)], scale=-0.5)
        k.stt(hn[:, :], x_[:, :], C["rs"][:, t:t + 1], gb[:, :], ALU.mult, ALU.mult, [x_, (C["rs"], t), gb], [hn])

    def tr_tile(i, t):
        hn = C["hn"][t % 2]
        aT = actT[i % 2]
        ptr = pb[7]
        pv = ptr.ap.bitcast(BF16)
        for c in range(8):
            k.tr(pv[:, c * 128:(c + 1) * 128], hn[:, c * 128:(c + 1) * 128], C["idb"][:, :], [hn, C["idb"]], [ptr])
        k.copy("act", aT[:, :, t * 128:(t + 1) * 128], pv.rearrange("p (c n) -> p c n", c=8), [ptr], [(aT, t)])

    def up(i):
        s, pos = divmod(i * TC, S)
        aT = actT[i % 2]

        def tail(j):
            b = j % 2
            k.act(accg[b][:, :], accg[b][:, :], AF.Silu, [accg[b]], [accg[b]])
            k.tt("dve", hid[:, j, :], accg[b][:, :], accu[b][:, :], ALU.mult, [accg[b], accu[b]], [(hid, j)])

        for j in range(22):
            b = j % 2
            raws_j = []
            for half, jj, acc in ((0, j, accg[b]), (1, 22 + j, accu[b])):
                pu = pb[2 + (2 * j + half) % 4]
                for c in range(8):
                    k.mm(pu[:, 0:TC], Wup[:, c, jj * 128:(jj + 1) * 128], aT[:, c, :], c == 0, c == 7, [(Wup, ublk(jj)), aT], [pu])
                raw = raws[rc[0] % 4]
                rc[0] += 1
                raws_j.append(raw)
                if pos == 0:
                    k.memset("pool", raw[:, 0:2], 0.0, [raw])
                else:
                    k.copy("act", raw[:, 0:2], halo[:, jj, :], [(halo, jj)], [raw])
                k.copy("act", raw[:, 2:TC + 2], pu[:, 0:TC], [pu], [raw])
                k.copy("act", halo[:, jj, :], raw[:, TC:TC + 2], [raw], [(halo, jj)])
                k.act(acc[:, :], pu[:, 0:TC], AF.Identity, [pu, fcw, fcb], [acc], scale=fcw[:, jj, 2:3], bias=fcb[:, jj:jj + 1])
            if j > 0:
                tail(j - 1)
            for (half, jj, acc), raw in zip(((0, j, accg[b]), (1, 22 + j, accu[b])), raws_j):
                for q_ in range(0, 2):
                    k.stt(acc[:, :], raw[:, q_:q_ + TC], fcw[:, jj, q_:q_ + 1], acc[:, :], ALU.mult, ALU.add, [raw, fcw, acc], [acc])
        tail(21)

    def down_tile(i, t):
        s, r0 = rows_of(i, t)
        x_ = xtl[xc[0] % 3]
        xc[0] += 1
        k.dma("sp", x_[:, :], xin[s, r0:r0 + 128, :], reads=[xin], writes=[x_], lane=f"xl{xc[0] % 3}")
        for hf in range(2):
            po = (pb[0], pb[1], pb[6])[(2 * t + hf) % 3]
            for j in range(22):
                k.mm(po[:, :], hid[:, j, t * 128:(t + 1) * 128], Wdn[:, j, hf * 512:(hf + 1) * 512], j == 0, j == 21, [(hid, j), Wdn], [po])
            k.tt("dve", x_[:, hf * 512:(hf + 1) * 512], x_[:, hf * 512:(hf + 1) * 512], po[:, :], ALU.add, [x_, po], [x_])
        k.dma("sp", xout[s, r0:r0 + 128, :], x_[:, :], reads=[x_], writes=[(xout, (s, r0))], lane=f"xst{xc[0] % 3}")

    for t in range(NT):
        norm_tile(0, t)
        tr_tile(0, t)
    for i in range(nchunk):
        up(i)
        for t in range(NT):
            if i + 1 < nchunk:
                norm_tile(i + 1, t)
            down_tile(i, t)
            if i + 1 < nchunk:
                tr_tile(i + 1, t)


def phase_A1(k, C, G):
    k.phase("A1")
    pb = k.banks
    Win = k.sb("Win", [128, 8, 800], BF16)
    load_w(k, Win, G["w_in1"], 8, "w0")
    Wuq = k.sb("Wuq", [128, 4, 1536], BF16)
    load_w(k, Wuq, G["w_uq"], 4, "w1")
    Wkv = k.sb("Wkv", [128, 2, 2048], BF16)
    load_w(k, Wkv, G["w_ukv"], 2, "w2")
    WqN = k.sb("WqN", [128, 4, 1024], BF16); WqR = k.sb("WqR", [128, 4, 512], BF16); WkN = k.sb("WkN", [128, 2, 1024], BF16)
    for c in range(4):
        wv_ = Wuq[:, c, :].rearrange("p (h d) -> p h d", h=16)
        k.copy("pool", WqN[:, c, :].rearrange("p (h d) -> p h d", h=16), wv_[:, :, 0:64], [Wuq], [WqN])
        k.copy("pool", WqR[:, c, :].rearrange("p (h d) -> p h d", h=16), wv_[:, :, 64:96], [Wuq], [WqR])
    for c in range(2):
        k.copy("pool", WkN[:, c, :].rearrange("p (h d) -> p h d", h=16), Wkv[:, c, :].rearrange("p (h d) -> p h d", h=16)[:, :, 0:64], [Wkv], [WkN])
    gb = k.sb("gb", [128, D])
    k.dma("sp", gb[:, :], G["mixg1"][:, :].broadcast_to([128, D]), writes=[gb], lane="c1")
    cosT = k.sb("cosT", [128, S]); sinT = k.sb("sinT", [128, S])
    k.dma("sp", cosT[:, :], G["cosR"][:, :], writes=[cosT], lane="c2")
    k.dma("sp", sinT[:, :], G["sinR"][:, :], writes=[sinT], lane="c3")
    cm = {}
    for i_, nm in enumerate(("R32", "bones", "bones32", "indRN0", "indRN1", "indNR0", "indNR1", "rep32", "onesF")):
        cm[nm] = k.sb(nm, [128, 128], BF16)
        k.dma("pool", cm[nm][:, :], G[nm][:, :], writes=[cm[nm]], lane=f"c{4 + i_ % 2}")
    onesb = cm["onesF"]
    gqa = k.sb("gqa", [128, 4]); gkva = k.sb("gkva", [128, 2])
    k.dma("sp", gqa[:, :], G["gqa"][:, :], writes=[gqa], lane="c1")
    k.dma("sp", gkva[:, :], G["gkva"][:, :], writes=[gkva], lane="c2")
    gv = {}
    for i_, nm in enumerate(("gqN", "gqR", "gkN", "gkR")):
        gv[nm] = k.sb(nm, [128, 1])
        k.dma("sp", gv[nm][:, :], G[nm][:, :], writes=[gv[nm]], lane=f"c{1 + i_ % 3}")
    xs2 = [k.sb(f"xs{i}", [128, 4, D]) for i in range(2)]
    hnT2 = [k.sb(f"hnT{i}", [128, 8, 512], BF16) for i in range(2)]
    hn4 = C["hn"] + [k.sb(f"hnx{i}", [128, 1024], BF16) for i in range(2)]
    craw = k.sb("craw", [128, 6, 512]); csq = k.sb("csq", [128, 6, 512], BF16)
    crs = k.sb("crs", [128, 2, 512])
    cn = k.sb("cn", [128, 6, 512], BF16)
    kpe = k.sb("kpe", [128, 512], BF16)
    k.memset("pool", kpe[:, :], 0.0, [kpe])
    W3 = lambda n, dt=F32: [[k.sb(f"{n}{p_}{i}", [128, 512], dt) for i in range(3)] for p_ in range(2)]
    sq = W3("sq", BF16); rstd = W3("rstd"); qn = W3("qn", BF16)
    t1 = [k.sb(f"t1{i}", [128, 512], BF16) for i in range(2)]
    t2 = [k.sb(f"t2{i}", [128, 512], BF16) for i in range(2)]
    qoR = [k.sb(f"qoR{i}", [128, 512], BF16) for i in range(2)]
    vo = [k.sb(f"vo{i}", [128, 16, 65], BF16) for i in range(2)]
    for b_ in range(2):
        k.memset("pool", vo[b_][:, :, 64:65], 1.0, [vo[b_]])
    gcount = [0]
    NCH = 2 * NSEQ * 2

    def xload(i):
        s, pos = i // 4, (i % 4) * 512
        k.dma("sp", xs2[i % 2][:, :, :], G["x2"][s, pos:pos + 512, :].rearrange("(t p) d -> p t d", p=128), reads=[G["x2"]], writes=[xs2[i % 2]], lane=f"xa{i % 2}")

    xload(0)
    for t in range(4):
        norm_stats(k, C, xs2[0], t, gb, hn4[t])
        norm_tr(k, C, hn4[t], hnT2[0], t)
    for i in range(NCH):
        s, pos = i // 4, (i % 4) * 512
        hnT = hnT2[i % 2]
        for cc in range(6):
            acc_b = pb[cc % 2]
            for c in range(8):
                k.mm(acc_b[:, :], Win[:, c, cc * 128:(cc + 1) * 128], hnT[:, c, :], c == 0, c == 7, [Win, hnT], [acc_b])
            k.copy("dve", craw[:, cc, :], acc_b[:, :], [acc_b], [(craw, cc)])
            k.act(csq[:, cc, :], acc_b[:, :], AF.Square, [acc_b], [(csq, cc)])
        pk = pb[7]
        for c in range(8):
            k.mm(pk[0:32, :], Win[:, c, 768:800], hnT[:, c, :], c == 0, c == 7, [Win, hnT], [pk])
        k.copy("act", kpe[0:32, :], pk[0:32, :], [pk], [kpe])
        for grp, (c0, c1, gg, nf) in enumerate(((0, 4, gqa, 512.0), (4, 6, gkva, 256.0))):
            pss = pb[5 + grp]
            for cc in range(c0, c1):
                k.mm(pss[:, :], onesb[:, :], csq[:, cc, :], cc == c0, cc == c1 - 1, [onesb, (csq, cc)], [pss])
            k.act(crs[:, grp, :], pss[:, :], AF.Ln, [pss, C["eps"]], [(crs, grp)], scale=1.0 / nf, bias=C["eps"][:, :])
            k.act(crs[:, grp, :], crs[:, grp, :], AF.Exp, [(crs, grp)], [(crs, grp)], scale=-0.5)
            for cc in range(c0, c1):
                k.stt(cn[:, cc, :], craw[:, cc, :], gg[:, cc - c0:cc - c0 + 1], crs[:, grp, :], ALU.mult, ALU.mult, [(craw, cc), gg, (crs, grp)], [(cn, cc)])
        for t in range(4):
            b = t % 2
            for hf in range(2):
                pvv = pb[4 + hf]
                wv = lambda c, hf=hf: Wkv[:, c, :].rearrange("p (h d) -> p h d", h=16)[:, hf * 8:(hf + 1) * 8, 64:128]
                for c in range(2):
                    k.mm(pvv[:, :].rearrange("p (h d) -> p h d", h=8), cn[:, 4 + c, t * 128:(t + 1) * 128], wv(c), c == 0, c == 1, [cn, Wkv], [pvv])
                k.copy("dve" if hf == 0 else "act", vo[b][:, hf * 8:(hf + 1) * 8, 0:64], pvv[:, :].rearrange("p (h d) -> p h d", h=8), [pvv], [(vo[b], hf)])
            r0 = pos + t * 128
            k.dma("sp", G["v1"][s, r0:r0 + 128, :], vo[b][:, :, :].rearrange("p h d -> p (h d)"), reads=[vo[b]], writes=[(G["v1"], (s, r0))], lane=f"vo{b}")
        gbase = gcount[0]
        gcount[0] += 8

        def stA(gi, gbase=gbase):
            par = (gbase + gi) % 2
            kind, g4 = gi // 4, gi % 4
            banks = pb[0:3] if par == 0 else pb[3:6]
            for ti in range(3):
                bank = banks[ti]
                if kind == 0:
                    if ti < 2:
                        c0_ = (4 * g4 + 2 * ti) * 64
                        for c in range(4):
                            k.mm(bank[:, :], WqN[:, c, c0_:c0_ + 128], cn[:, c, :], c == 0, c == 3, [WqN, cn], [bank])
                    else:
                        c0_ = 4 * g4 * 32
                        for c in range(4):
                            k.mm(bank[:, :], WqR[:, c, c0_:c0_ + 128], cn[:, c, :], c == 0, c == 3, [WqR, cn], [bank])
                else:
                    if ti < 2:
                        c0_ = (4 * g4 + 2 * ti) * 64
                        for c in range(2):
                            k.mm(bank[:, :], WkN[:, c, c0_:c0_ + 128], cn[:, 4 + c, :], c == 0, c == 1, [WkN, cn], [bank])
                    else:
                        k.mm(bank[:, :], cm["rep32"][:, :], kpe[:, :], True, True, [cm["rep32"], kpe], [bank])
                k.act(sq[par][ti][:, :], bank[:, :], AF.Square, [bank], [sq[par][ti]])

        def stB(gi, s=s, pos=pos, gbase=gbase):
            par = (gbase + gi) % 2
            kind, g4 = gi // 4, gi % 4
            banks = pb[0:3] if par == 0 else pb[3:6]
            gN, gR = (gv["gqN"], gv["gqR"]) if kind == 0 else (gv["gkN"], gv["gkR"])
            dst = G["qT1"] if kind == 0 else G["kT1"]
            sqs = sq[par]
            contrib = (
                ((cm["bones"], sqs[0]), (cm["indRN0"], sqs[2])),
                ((cm["bones"], sqs[1]), (cm["indRN1"], sqs[2])),
                ((cm["bones32"], sqs[2]), (cm["indNR0"], sqs[0]), (cm["indNR1"], sqs[1])),
            )
            for ti in range(3):
                pss = pb[6]
                lst = contrib[ti]
                for ci, (lh, sqt) in enumerate(lst):
                    k.mm(pss[:, :], lh[:, :], sqt[:, :], ci == 0, ci == len(lst) - 1, [lh, sqt], [pss])
                rs_ = rstd[par][ti]
                k.act(rs_[:, :], pss[:, :], AF.Ln, [pss, C["eps"]], [rs_], scale=1.0 / 96, bias=C["eps"][:, :])
                k.act(rs_[:, :], rs_[:, :], AF.Exp, [rs_], [rs_], scale=-0.5)
                g_ = gN if ti < 2 else gR
                k.stt(qn[par][ti][:, :], banks[ti][:, :], g_[:, 0:1], rs_[:, :], ALU.mult, ALU.mult, [banks[ti], g_, rs_], [qn[par][ti]])
            for ti in range(2):
                for hn_ in range(2):
                    h = 4 * g4 + 2 * ti + hn_
                    k.dma("sp", dst[s, h, 0:64, pos:pos + 512], qn[par][ti][hn_ * 64:(hn_ + 1) * 64, :], reads=[qn[par][ti]],
                          writes=[(dst, (s, h, 0, pos))], lane=f"qa{par}{2 * ti + hn_}")

        def stC(gi, s=s, pos=pos, gbase=gbase):
            par = (gbase + gi) % 2
            kind, g4 = gi // 4, gi % 4
            dst = G["qT1"] if kind == 0 else G["kT1"]
            prx = pb[7]
            qr = qn[par][2]
            k.mm(prx[:, :], cm["R32"][:, :], qr[:, :], True, True, [cm["R32"], qr], [prx])
            k.tt("pool", t1[par][:, :], qr[:, :], cosT[:, pos:pos + 512], ALU.mult, [qr, cosT], [t1[par]])
            k.tt("dve", t2[par][:, :], prx[:, :], sinT[:, pos:pos + 512], ALU.mult, [prx, sinT], [t2[par]])
            k.tt("dve", qoR[par][:, :], t1[par][:, :], t2[par][:, :], ALU.add, [t1[par], t2[par]], [qoR[par]])
            for hq in range(4):
                h = 4 * g4 + hq
                k.dma("sp", dst[s, h, 64:96, pos:pos + 512], qoR[par][hq * 32:(hq + 1) * 32, :], reads=[qoR[par]],
                      writes=[(dst, (s, h, 1, pos))], lane=f"qa{par}{4 + hq}")

        pipeline(8, [(0, stA), (1, stB), (2, stC)])
        if i + 1 < NCH:
            xload(i + 1)
            for t in range(4):
                norm_stats(k, C, xs2[(i + 1) % 2], t, gb, hn4[t])
        if i + 1 < NCH:
            for t in range(4):
                norm_tr(k, C, hn4[t], hnT2[(i + 1) % 2], t)


def phase_B1(k, C, G):
    k.phase("B1")
    cmask = k.sb("cmask", [128, 4, 512], BF16)
    k.dma("pool", cmask[:, :, :], G["cmask"][:, :, :], writes=[cmask], lane="c2")
    Pt = [k.sb(f"Pt{i}", [128, 512], BF16) for i in range(3)]
    omixs = [k.sb(f"omix{i}", [128, 16, 1024], BF16) for i in range(2)]
    vst = k.sb("vst", [128, 16, 16 * 65], BF16)
    C["rec"] = [k.sb(f"rec{i}", [128, 4]) for i in range(2)]
    ob = outproj_alloc(k, G, "w_out1")
    NB = 3
    kTs = [k.sb(f"kT{i}", [96, S], BF16) for i in range(NB)]
    qTs = [k.sb(f"qT{i}", [96, S], BF16) for i in range(NB)]
    n = 0
    for s in range(NSEQ):
        omix = omixs[s % 2]
        k.dma("sp", vst[:, :, :], G["v1"][s].rearrange("(t p) d -> p t d", p=128), reads=[G["v1"]], writes=[vst], lane="V")
        heads = []
        for h in range(16):
            b = n % NB
            n += 1
            kT, qT = kTs[b], qTs[b]

            def load(s=s, h=h, b=b, kT=kT, qT=qT):
                k.dma("sp", kT[:, :], G["kT1"][s, h, :, :], reads=[G["kT1"]], writes=[kT], lane=f"kT{b}")
                k.dma("sp", qT[:, :], G["qT1"][s, h, :, :], reads=[G["qT1"]], writes=[qT], lane=f"qT{b}")

            heads.append({"kT": kT, "qT": qT, "Va": vst, "omix": omix, "ocol": h * 64, "load": load,
                          "k": (lambda kt, kT=kT: kT[:, kt * 128:(kt + 1) * 128]),
                          "q": (lambda c, j0, qT=qT: qT[:, c * 512 + j0:(c + 1) * 512]),
                          "v": (lambda kt, h=h: vst[:, kt, h * 65:(h + 1) * 65])})
        heads[0]["load"]()
        heads[1]["load"]()
        for h in range(16):
            fns = []
            if h + 2 < 16:
                fns.append(heads[h + 2]["load"])
            if s > 0:
                po_, so_ = omixs[(s - 1) % 2], s - 1
                fns.append(lambda h=h, po_=po_, so_=so_: outproj_seq(
                    k, C, so_, (lambda tt, c: (po_[:, tt, c * 128:(c + 1) * 128], po_)), ob, G["x2"], G["x3"], tiles=[h]))
            heads[h]["after"] = (lambda fns=fns: [f() for f in fns])
        attention_run(k, C, heads, 96.0 ** -0.5, cmask, Pt)
        if s == NSEQ - 1:
            src = lambda tt, c, omix=omix: (omix[:, tt, c * 128:(c + 1) * 128], omix)
            outproj_seq(k, C, s, src, ob, G["x2"], G["x3"])


INTER = {
    "qT0": ([NSEQ, 512, S], BF16), "kT0": ([NSEQ, 512, S], BF16), "v0": ([NSEQ, S, 8 * 65], BF16),
    "z0": ([NSEQ, S, 512], BF16), "xbcT": ([NSEQ, 1024, S], BF16), "dt0": ([NSEQ, 128, 16, 8], F32),
    "mix0": ([NSEQ, S, D], BF16), "x1": ([NSEQ, S, D], F32), "x2": ([NSEQ, S, D], F32), "x3": ([NSEQ, S, D], F32),
    "qT1": ([NSEQ, 16, 96, S], BF16), "kT1": ([NSEQ, 16, 96, S], BF16), "v1": ([NSEQ, S, 16 * 65], BF16),
    "mix1": ([NSEQ, S, D], BF16),
}
ALL_PHASES = ("A0", "B0", "C0", "D0", "A1", "B1", "D1")


def build(phases=ALL_PHASES, kinds=None, cshapes=None, pshapes=None):
    kinds = kinds or {}
    nc = bass.Bass("TRN2", target_bir_lowering=False)
    k = KB(nc)
    G = {}

    def dram(name, shape, dt, default):
        G[name] = T(name, nc.dram_tensor(name, list(shape), dt, kind=kinds.get(name, default)).ap())

    dram("x", [NSEQ, S, D], F32, "ExternalInput")
    dram("y", [NSEQ, S, D], F32, "ExternalOutput")
    for name, (shape, dt) in INTER.items():
        dram(name, shape, dt, "Internal")
    for name, shape in cshapes.items():
        dram(name, shape, F32, "ExternalInput")
    for name, shape in pshapes.items():
        dram(name, shape, F32, "ExternalInput")
    C = common_consts(k, G)
    for ph in phases:
        if ph == "A0":
            phase_A0(k, C, G)
        elif ph == "B0":
            phase_B0(k, C, G)
        elif ph == "C0":
            phase_C0(k, C, G)
        elif ph == "D0":
            phase_D(k, C, G, 0, G["x1"], G["x2"])
        elif ph == "A1":
            phase_A1(k, C, G)
        elif ph == "B1":
            phase_B1(k, C, G)
        elif ph == "D1":
            phase_D(k, C, G, 1, G["x3"], G["y"])
    k.barrier()
    k.emit()
    return nc, k


def kernel(**inputs):
    consts = host_consts()
    params = host_params(inputs)
    nc, _ = build(cshapes={n: a.shape for n, a in consts.items()}, pshapes={n: a.shape for n, a in params.items()})
    x = np.ascontiguousarray(np.asarray(inputs["x"], dtype=np.float32))
    in_maps = []
    for i in range(8):
        m = {"x": x[2 * i:2 * i + 2]}
        m.update(consts)
        m.update(params)
        in_maps.append(m)
    res = run_bass_kernel_spmd(nc, in_maps, core_ids=list(range(8)))
    return np.concatenate([np.asarray(r["y"], dtype=np.float32) for r in res.results], axis=0)
```
